# Optimizing a Trainium2 kernel written in Bass

```python
import math
import jax, jax.numpy as jnp
from jax import lax
import numpy as np

D_MODEL = 1024
BATCH = 16
SEQ = 2048
DEPTH = 2

HEAD_DIM = 64
N_SB_HEADS = 4
N_FOX_HEADS = 4
N_DIL_HEADS = 4
N_LRU_BLOCKS = 4
LRU_BLOCK = HEAD_DIM
LRU_WIDTH = N_LRU_BLOCKS * LRU_BLOCK
SB_W = N_SB_HEADS * HEAD_DIM
FOX_W = N_FOX_HEADS * HEAD_DIM
DIL_W = N_DIL_HEADS * HEAD_DIM
MIX_WIDTH = SB_W + FOX_W + DIL_W + LRU_WIDTH
N_IN = 3 * SB_W + 3 * FOX_W + N_FOX_HEADS + 3 * DIL_W + 2 * LRU_WIDTH
BLOCK = 128
DILATED_PATTERNS = ((128, 1), (512, 4), (2048, 16))
LRU_CONV_WIDTH = 4
LRU_C = 8.0
FFN_CONV_WIDTH = 3
D_FF = 2816
N_MEM = 256
N_CROSS_HEADS = 4
CROSS_W = N_CROSS_HEADS * HEAD_DIM
NUM_BUCKETS = 32
MAX_DISTANCE = 2048
EPS = 1e-6

kernel_name = "hymba_style_sb_fox_dilated_rglru_trunk"


def rms_norm(x, g):
    x32 = x.astype(jnp.float32)
    y = x32 * lax.rsqrt(jnp.mean(x32 * x32, axis=-1, keepdims=True) + EPS)
    return (y * g.astype(jnp.float32)).astype(x.dtype)


def causal_dwconv(x, w, b):
    k_w = w.shape[0]
    s = x.shape[1]
    xp = jnp.pad(x, ((0, 0), (k_w - 1, 0), (0, 0)))
    y = b
    for j in range(k_w):
        y = y + w[j] * xp[:, j:j + s]
    return y


def t5_bucket(dist):
    n = jnp.maximum(dist, 0)
    max_exact = NUM_BUCKETS // 2
    nf = jnp.maximum(n, 1).astype(jnp.float32)
    large = max_exact + (jnp.log(nf / max_exact) / math.log(MAX_DISTANCE / max_exact)
                         * (NUM_BUCKETS - max_exact)).astype(jnp.int32)
    large = jnp.minimum(large, NUM_BUCKETS - 1)
    return jnp.where(n < max_exact, n, large)


def split_cols(t, sizes):
    out, start = [], 0
    for n in sizes:
        out.append(t[..., start:start + n])
        start += n
    return out


def split_qkv(t, n_heads):
    b, s, _ = t.shape
    t = t.reshape(b, s, 3, n_heads, HEAD_DIM).transpose(2, 0, 3, 1, 4)
    return t[0] * HEAD_DIM ** -0.5, t[1], t[2]


def merge_heads(o):
    b, h, s, d = o.shape
    return o.transpose(0, 2, 1, 3).reshape(b, s, h * d)


def stick_breaking_attention(q, k, v):
    s_len = q.shape[2]
    outs = []
    for i in range(s_len // BLOCK):
        q0, q1 = i * BLOCK, (i + 1) * BLOCK
        z = jnp.einsum('bhqd,bhkd->bhqk', q[:, :, q0:q1], k[:, :, :q1]).astype(jnp.float32)
        strict = jnp.arange(q1)[None, :] < jnp.arange(q0, q1)[:, None]
        log_keep = jnp.where(strict, jax.nn.log_sigmoid(-z), 0.0)
        log_keep_after = lax.cumsum(log_keep, axis=3, reverse=True) - log_keep
        att = jnp.where(strict, jnp.exp(jax.nn.log_sigmoid(z) + log_keep_after), 0.0)
        outs.append(jnp.einsum('bhqk,bhkd->bhqd', att, v[:, :, :q1].astype(jnp.float32)))
    return jnp.concatenate(outs, axis=2)


def forgetting_attention(q, k, v, f_logit):
    s_len = q.shape[2]
    log_f = jax.nn.log_sigmoid(f_logit.astype(jnp.float32))
    cum = jnp.cumsum(log_f, axis=1).transpose(0, 2, 1)
    outs = []
    for i in range(s_len // BLOCK):
        q0, q1 = i * BLOCK, (i + 1) * BLOCK
        z = jnp.einsum('bhqd,bhkd->bhqk', q[:, :, q0:q1], k[:, :, :q1]).astype(jnp.float32)
        z = z + cum[:, :, q0:q1, None] - cum[:, :, None, :q1]
        causal = jnp.arange(q1)[None, :] <= jnp.arange(q0, q1)[:, None]
        p = jax.nn.softmax(jnp.where(causal, z, -jnp.inf), axis=-1)
        outs.append(jnp.einsum('bhqk,bhkd->bhqd', p, v[:, :, :q1].astype(jnp.float32)))
    return jnp.concatenate(outs, axis=2)


def dilated_branch(q, k, v, rel_bias, dil, steps):
    b, h, s, hd = q.shape
    L = s // dil
    qb_len = math.gcd(BLOCK, L)
    nb = L // qb_len

    def to_cls(t):
        return t.reshape(b, h, L, dil, hd).transpose(0, 1, 3, 2, 4)

    qc = to_cls(q).reshape(b, h, dil, nb, qb_len, hd)
    pad = ((0, 0), (0, 0), (0, 0), (steps, 0), (0, 0))
    kc = jnp.pad(to_cls(k), pad)
    vc = jnp.pad(to_cls(v), pad)
    idx = (jnp.arange(nb) * qb_len)[:, None] + jnp.arange(qb_len + steps)[None, :]
    kb = kc[:, :, :, idx]
    vb = vc[:, :, :, idx]
    sc = jnp.einsum('bhrnqd,bhrnkd->bhrnqk', qc, kb).astype(jnp.float32)
    qpos = (jnp.arange(nb) * qb_len)[:, None] + jnp.arange(qb_len)[None, :]
    kpos = idx - steps
    delta = qpos[:, :, None] - kpos[:, None, :]
    valid = (delta >= 0) & (delta <= steps) & (kpos[:, None, :] >= 0)
    bias = jnp.moveaxis(rel_bias.astype(jnp.float32)[t5_bucket(delta * dil)], -1, 0)[:, None]
    sc = jnp.where(valid, sc + bias, -jnp.inf)
    m = jnp.max(sc, axis=-1, keepdims=True)
    p = jnp.exp(sc - m)
    l = jnp.sum(p, axis=-1)
    o = jnp.einsum('bhrnqk,bhrnkd->bhrnqd', p, vb.astype(jnp.float32)) / l[..., None]
    lse = m[..., 0] + jnp.log(l)
    o = o.reshape(b, h, dil, L, hd).transpose(0, 1, 3, 2, 4).reshape(b, h, s, hd)
    lse = lse.reshape(b, h, dil, L).transpose(0, 1, 3, 2).reshape(b, h, s)
    return o, lse


def dilated_attention(q, k, v, rel_bias):
    outs, lses = [], []
    for window, dil in DILATED_PATTERNS:
        o, lse = dilated_branch(q, k, v, rel_bias, dil, window // dil)
        outs.append(o)
        lses.append(lse)
    wts = jax.nn.softmax(jnp.stack(lses), axis=0)
    return jnp.sum(wts[..., None] * jnp.stack(outs), axis=0)


def rg_lru_branch(x_br, gate_br, conv_w, conv_b, w_a, b_a, w_x, b_x, lam):
    b, s, c = x_br.shape
    f32 = jnp.float32
    xc = causal_dwconv(x_br.astype(f32), conv_w.astype(f32), conv_b.astype(f32))
    xg = xc.reshape(b, s, N_LRU_BLOCKS, LRU_BLOCK)
    r = jax.nn.sigmoid(jnp.einsum('bsgi,gij->bsgj', xg, w_a.astype(f32)).reshape(b, s, c) + b_a.astype(f32))
    i_gate = jax.nn.sigmoid(jnp.einsum('bsgi,gij->bsgj', xg, w_x.astype(f32)).reshape(b, s, c) + b_x.astype(f32))
    log_a = -LRU_C * r * jax.nn.softplus(-lam.astype(f32))
    a = jnp.exp(log_a)
    u = jnp.sqrt(-jnp.expm1(2.0 * log_a)) * (i_gate * xc)

    def combine(e1, e2):
        a1, b1 = e1
        a2, b2 = e2
        return a1 * a2, a2 * b1 + b2

    _, h = lax.associative_scan(combine, (a, u), axis=1)
    return h * jax.nn.gelu(gate_br.astype(f32), approximate=False)


def memory_cross_attention(h, mem_n, w_cq, w_ck, w_cv, w_co):
    b, s, _ = h.shape
    m = mem_n.shape[1]
    q = (h @ w_cq).reshape(b, s, N_CROSS_HEADS, HEAD_DIM).transpose(0, 2, 1, 3) * HEAD_DIM ** -0.5
    k = (mem_n @ w_ck).reshape(b, m, N_CROSS_HEADS, HEAD_DIM).transpose(0, 2, 1, 3)
    v = (mem_n @ w_cv).reshape(b, m, N_CROSS_HEADS, HEAD_DIM).transpose(0, 2, 1, 3)
    p = jax.nn.softmax(jnp.einsum('bhqd,bhkd->bhqk', q, k).astype(jnp.float32), axis=-1)
    o = jnp.einsum('bhqk,bhkd->bhqd', p, v.astype(jnp.float32))
    return merge_heads(o).astype(h.dtype) @ w_co


def setup_inputs(seed: int = 0) -> dict:
    key = jax.random.key(seed)
    ks = iter(jax.random.split(key, 40))
    f32 = jnp.float32
    L = DEPTH

    def normal(shape, scale):
        return scale * jax.random.normal(next(ks), shape, f32)

    def gain(shape):
        return 1.0 + normal(shape, 0.02)

    u = jax.random.uniform(next(ks), (L, LRU_WIDTH), f32, 0.9, 0.999)
    a0 = u ** (1.0 / LRU_C)
    lru_lambda = jnp.log(a0) - jnp.log1p(-a0)
    return {
        "x": normal((BATCH, SEQ, D_MODEL), 1.0),
        "mem": normal((BATCH, N_MEM, D_MODEL), 1.0),
        "norm_mix_g": gain((L, D_MODEL)),
        "w_in": normal((L, D_MODEL, N_IN), D_MODEL ** -0.5),
        "b_forget": jax.random.uniform(next(ks), (L, N_FOX_HEADS), f32, 1.0, 4.0),
        "lru_conv_w": normal((L, LRU_CONV_WIDTH, LRU_WIDTH), LRU_CONV_WIDTH ** -0.5),
        "lru_conv_b": normal((L, LRU_WIDTH), 0.01),
        "lru_w_a": normal((L, N_LRU_BLOCKS, LRU_BLOCK, LRU_BLOCK), LRU_BLOCK ** -0.5),
        "lru_b_a": normal((L, LRU_WIDTH), 0.01),
        "lru_w_x": normal((L, N_LRU_BLOCKS, LRU_BLOCK, LRU_BLOCK), LRU_BLOCK ** -0.5),
        "lru_b_x": normal((L, LRU_WIDTH), 0.01),
        "lru_lambda": lru_lambda,
        "w_out": normal((L, MIX_WIDTH, D_MODEL), MIX_WIDTH ** -0.5),
        "norm_cross_g": gain((L, D_MODEL)),
        "norm_mem_g": gain((L, D_MODEL)),
        "w_cq": normal((L, D_MODEL, CROSS_W), D_MODEL ** -0.5),
        "w_ck": normal((L, D_MODEL, CROSS_W), D_MODEL ** -0.5),
        "w_cv": normal((L, D_MODEL, CROSS_W), D_MODEL ** -0.5),
        "w_co": normal((L, CROSS_W, D_MODEL), CROSS_W ** -0.5),
        "norm_ffn_g": gain((L, D_MODEL)),
        "w_up": normal((L, D_MODEL, 2 * D_FF), D_MODEL ** -0.5),
        "ffn_conv_w": normal((L, FFN_CONV_WIDTH, 2 * D_FF), FFN_CONV_WIDTH ** -0.5),
        "ffn_conv_b": normal((L, 2 * D_FF), 0.01),
        "w_down": normal((L, D_FF, D_MODEL), D_FF ** -0.5),
        "rel_bias": normal((NUM_BUCKETS, N_DIL_HEADS), 0.5),
        "final_norm_g": gain((D_MODEL,)),
    }


def reference(x, mem, norm_mix_g, w_in, b_forget, lru_conv_w, lru_conv_b, lru_w_a, lru_b_a,
              lru_w_x, lru_b_x, lru_lambda, w_out, norm_cross_g, norm_mem_g, w_cq, w_ck, w_cv,
              w_co, norm_ffn_g, w_up, ffn_conv_w, ffn_conv_b, w_down, rel_bias, final_norm_g):
    col_sizes = [3 * SB_W, 3 * FOX_W, N_FOX_HEADS, 3 * DIL_W, LRU_WIDTH, LRU_WIDTH]
    for l in range(DEPTH):
        h = rms_norm(x, norm_mix_g[l])
        proj = h @ w_in[l]
        sb_qkv, fox_qkv, fox_f, dil_qkv, lru_x, lru_gate = split_cols(proj, col_sizes)
        o_sb = stick_breaking_attention(*split_qkv(sb_qkv, N_SB_HEADS))
        o_fox = forgetting_attention(*split_qkv(fox_qkv, N_FOX_HEADS), fox_f + b_forget[l])
        o_dil = dilated_attention(*split_qkv(dil_qkv, N_DIL_HEADS), rel_bias)
        o_lru = rg_lru_branch(lru_x, lru_gate, lru_conv_w[l], lru_conv_b[l], lru_w_a[l],
                              lru_b_a[l], lru_w_x[l], lru_b_x[l], lru_lambda[l])
        mixed = jnp.concatenate([merge_heads(o_sb), merge_heads(o_fox), merge_heads(o_dil), o_lru],
                                axis=-1).astype(x.dtype)
        x = x + mixed @ w_out[l]
        x = x + memory_cross_attention(rms_norm(x, norm_cross_g[l]), rms_norm(mem, norm_mem_g[l]),
                                       w_cq[l], w_ck[l], w_cv[l], w_co[l])
        hf = rms_norm(x, norm_ffn_g[l]) @ w_up[l]
        hf = causal_dwconv(hf, ffn_conv_w[l], ffn_conv_b[l])
        up, gate = hf[..., :D_FF], hf[..., D_FF:]
        x = x + (jax.nn.gelu(gate, approximate=False) * up).astype(x.dtype) @ w_down[l]
    return rms_norm(x, final_norm_g)
```

```python
import math
import numpy as np
from contextlib import ExitStack
import concourse.bass as bass
import concourse.mybir as mybir
from concourse.bass_utils import run_bass_kernel_spmd

F32 = mybir.dt.float32
BF16 = mybir.dt.bfloat16
AF = mybir.ActivationFunctionType
ALU = mybir.AluOpType
AX = mybir.AxisListType

NDSEM = 48


class Res:
    __slots__ = ("name", "w", "rs")

    def __init__(self, name=""):
        self.name = name
        self.w = None
        self.rs = {}


class Sched:
    ENG = ("pe", "act", "dve", "pool", "sp")

    def __init__(self, nc, es):
        self.nc = nc
        self.sem = {e: es.enter_context(nc.semaphore("s_" + e)) for e in self.ENG}
        self.dsem = [es.enter_context(nc.semaphore("d%d" % i)) for i in range(NDSEM)]
        self.cnt = {e: 0 for e in self.ENG}
        self.ops = {e: [] for e in self.ENG}
        self.seen = {e: {} for e in self.ENG}
        self.dcnt = [0] * NDSEM
        self.dnext = 0

    def _collect(self, eng, reads, writes):
        waits = {}
        seen = self.seen[eng]

        def need(dep, same_ok):
            if dep is None:
                return
            k, v = dep
            if k == eng and same_ok:
                return
            if seen.get(k, 0) >= v:
                return
            if waits.get(k, 0) < v:
                waits[k] = v
        for r in reads:
            need(r.w, eng == "pe")
        for w in writes:
            need(w.w, eng == "pe")
            for d in w.rs.items():
                need(d, eng == "pe")
        for k, v in waits.items():
            seen[k] = v
        return list(waits.items())

    def op(self, eng, fn, reads=(), writes=()):
        waits = self._collect(eng, reads, writes)
        self.cnt[eng] += 1
        me = (eng, self.cnt[eng])
        self.ops[eng].append((fn, waits, None))
        for r in reads:
            if r.rs.get(eng, 0) < me[1]:
                r.rs[eng] = me[1]
        for w in writes:
            w.w = me
            w.rs = {}

    def dma(self, q, out, in_, reads=(), writes=(), **kw):
        idx = self.dnext
        self.dnext = (self.dnext + 1) % NDSEM
        prev = self.dcnt[idx]
        self.dcnt[idx] += 16
        val = self.dcnt[idx]
        waits = self._collect(q, reads, writes)
        key = "d%d" % idx
        if prev > 0 and self.seen[q].get(key, 0) < prev:
            waits.append((key, prev))
            self.seen[q][key] = prev
        self.ops[q].append((lambda e: e.dma_start(out=out, in_=in_, **kw), waits, idx))
        for r in reads:
            r.rs[key] = val
        for w in writes:
            w.w = (key, val)
            w.rs = {}

    def handle(self, k):
        if k in self.sem:
            return self.sem[k]
        return self.dsem[int(k[1:])]

    def finish(self):
        waits = []
        for i in range(NDSEM):
            if self.dcnt[i] > 0:
                waits.append(("d%d" % i, self.dcnt[i]))
        for e in self.ENG:
            if e != "sp" and self.cnt[e] > 0:
                waits.append((e, self.cnt[e]))
        self.ops["sp"].append((None, waits, None))

    def barrier(self):
        for e in self.ENG:
            waits = []
            for i in range(NDSEM):
                k = "d%d" % i
                if self.dcnt[i] > 0 and self.seen[e].get(k, 0) < self.dcnt[i]:
                    waits.append((k, self.dcnt[i]))
                    self.seen[e][k] = self.dcnt[i]
            for o in self.ENG:
                if o != e and self.cnt[o] > 0 and self.seen[e].get(o, 0) < self.cnt[o]:
                    waits.append((o, self.cnt[o]))
                    self.seen[e][o] = self.cnt[o]
            if waits:
                self.ops[e].append((None, waits, None))

    def emit(self):
        nc = self.nc
        with nc.Block() as block:
            def mk(e):
                def body(engobj):
                    for fn, waits, dm in self.ops[e]:
                        for k, v in waits:
                            engobj.wait_ge(self.handle(k), v)
                        if fn is None:
                            continue
                        ins = fn(engobj)
                        if dm is None:
                            ins.then_inc(self.sem[e], 1)
                        else:
                            ins.then_inc(self.dsem[dm], 16)
                return body
            block.tensor(mk("pe"))
            block.scalar(mk("act"))
            block.vector(mk("dve"))
            block.gpsimd(mk("pool"))
            block.sync(mk("sp"))

L = 2; SEQ = 2048; DM = 1024; NT = 4; TT = 512; NIN = 2820; DFF = 2816; NMEM = 256
OFF_SB, OFF_FOX, OFF_F, OFF_DIL, OFF_LX, OFF_LG = 0, 768, 1536, 1540, 2308, 2564
WCOLS = 6144
EW = 2432
TP = 2560
NWORK = 6
G_MIX, G_CROSS, G_MEM, G_FFN = 0, 1, 2, 3


def build_nc(nseq=2, nlayers=L, dbg=False, plan=None, rec=None):
    nc = bass.Bass("TRN2", target_bir_lowering=False)

    def din(name, shape, dt=F32):
        return nc.dram_tensor(name, shape, dt, kind="ExternalInput").ap()
    xT_d = din("xT", [2, DM, SEQ]); memT_d = din("memT", [2, DM, NMEM])
    w_in_d = din("w_in", [L, DM, NIN]); w_out_d = din("w_out", [L, DM, DM])
    w_cq_d = din("w_cq", [L, DM, 256]); w_ck_d = din("w_ck", [L, DM, 256]); w_cv_d = din("w_cv", [L, DM, 256])
    w_co_d = din("w_co", [L, 256, DM]); w_up_d = din("w_up", [L, DM, 2 * DFF]); w_down_d = din("w_down", [L, DFF, DM])
    gains_d = din("gains", [128, (4 * L + 1) * 8]); fb_d = din("fb", [4, L])
    lrup_d = din("lrup", [128, L * 2 * 8]); wabd_d = din("wabd", [L, 2, 128, 128]); wxbd_d = din("wxbd", [L, 2, 128, 128])
    ffnp_d = din("ffnp", [128, L * 44 * 4]); relb_d = din("relb", [32, 4])
    ident_d = din("ident", [128, 128]); tri_d = din("tri", [128, 128]); mask_d = din("mask", [128, 897])
    onehot_d = din("onehot", [32, TP]); mult_d = din("mult", [128, TP]); sel_d = din("sel", [4, 4 * 128])
    yT_d = nc.dram_tensor("yT", [2, DM, SEQ], F32, kind="ExternalOutput").ap()
    tpad_d = nc.dram_tensor("tpad", [4, 128, TP], BF16, kind="Internal").ap()
    wup_s = nc.dram_tensor("wup_s", [L, 8, 128, WCOLS], BF16, kind="Internal").ap()
    wdn_s = nc.dram_tensor("wdn_s", [L, 8, 128, 22 * 128], BF16, kind="Internal").ap()
    dbg_d = {}
    if dbg:
        for nm in ("d_h", "d_mixed", "d_x1", "d_x2", "d_x3"):
            dbg_d[nm] = nc.dram_tensor(nm, [DM, SEQ], F32, kind="ExternalOutput").ap()

    es = ExitStack()
    with es:
        S = Sched(nc, es)

        def sb(name, shape, dt):
            return es.enter_context(nc.sbuf_tensor(name, shape, dt))
        xT = sb("xTt", [128, 8, SEQ], F32); rx = [[Res() for _ in range(NT)] for _ in range(8)]
        hT = sb("hTt", [128, 8, SEQ], BF16); rh = [[Res() for _ in range(NT)] for _ in range(8)]
        mixedT = sb("mixT", [128, 8, SEQ], BF16); rm = [[Res() for _ in range(NT)] for _ in range(8)]
        qT = sb("qTt", [128, SEQ], BF16); rq = [Res() for _ in range(NT)]
        kT = sb("kTt", [128, SEQ], BF16); rk = [Res() for _ in range(NT)]
        vaug = sb("vaug", [128, 16, 2, 128], BF16); rv = [Res() for _ in range(4)]
        xr = qT[:, 0:1032].bitcast(F32); gr = kT[:, 0:1032].bitcast(F32); rxr = rq; rgr = rk
        wbuf = [sb("wbuf%d" % i, [128, WCOLS], BF16) for i in range(2)]; rw = [Res(), Res()]
        work = sb("work", [128, NWORK, TT], F32); rwk = [Res() for _ in range(NWORK)]
        pbf = sb("pbf", [128, 6, TT], BF16); rpbf = [Res() for _ in range(6)]
        estrip = sb("estrip", [128, EW], BF16); res_ = Res()
        memn = sb("memn", [128, 8, NMEM], BF16); rmemn = Res()
        rrow = sb("rrow", [1, TT], F32); rrr = Res()
        rbf = sb("rbf", [128, TT], BF16); rrb = Res()
        ones_r0 = sb("ones_r0", [128, 128], BF16)
        selb = sb("selb", [128, 4 * 128], BF16)
        ones_bf = sb("ones_bf", [128, 128], BF16); ones_f = sb("ones_f", [128, 128], F32)
        tri = sb("trib", [128, 128], BF16); ident = sb("identf", [128, 128], F32)
        maskb = sb("maskb", [128, 897], BF16)
        gains = sb("gainst", [128, (4 * L + 1) * 8], F32); fbn = sb("fbn", [4, L], F32)
        lrup = sb("lrupt", [128, L * 2 * 8], F32); coef = sb("coef", [128, 2], F32); hlast = sb("hlast", [128, 2], F32)
        wabd = sb("wabdt", [128, 2, 128], F32); wxbd = sb("wxbdt", [128, 2, 128], F32); rwab = Res()
        ffnp = sb("ffnpt", [128, L * 44 * 4], F32); fhalo = sb("fhalo", [128, 44, 2], F32); rfh = Res()
        ckneg = sb("ckneg", [128, 64], F32); rck = Res()
        cst = sb("cst", [128, 2], F32)
        stage = sb("stage", [32, 136], F32); rstage = Res()
        rconst = Res(); rcoef = Res(); rhl = [Res(), Res()]; rtpad = Res()
        pb = [es.enter_context(nc.psum_tensor("pb%d" % i, [128, TT], F32)) for i in range(8)]
        rpb = [Res() for _ in range(8)]

        ctr = {"b": 0, "w": 0, "p": 0, "wb": 0, "ld": 0}
        rs_up = [[Res() for _ in range(8)] for _ in range(L)]
        rs_dn = [[Res() for _ in range(8)] for _ in range(L)]
        FGROUPS = [(g * 3, min(3, 22 - g * 3)) for g in range(8)]
        convq = []

        def conv_jobs(l):
            jobs = []
            for g, (j0, n) in enumerate(FGROUPS):
                def job(g=g, j0=j0, n=n):
                    dst = wup_s[l, g].rearrange("p (c n) -> p c n", c=8)
                    S.dma("pool", dst[:, :, 0:n * 128], w_up_d[l][:, j0 * 128:(j0 + n) * 128].rearrange("(c p) n -> p c n", p=128),
                          writes=[rs_up[l][g]])
                    S.dma("pool", dst[:, :, 384:384 + n * 128],
                          w_up_d[l][:, DFF + j0 * 128:DFF + (j0 + n) * 128].rearrange("(c p) n -> p c n", p=128), writes=[rs_up[l][g]])
                jobs.append(job)
            for m in range(8):
                def job(m=m):
                    S.dma("pool", wdn_s[l, m].rearrange("p (c n) -> p c n", c=22),
                          w_down_d[l][:, m * 128:(m + 1) * 128].rearrange("(c p) n -> p c n", p=128), writes=[rs_dn[l][m]])
                jobs.append(job)
            return jobs

        def drip(k):
            for _ in range(k):
                if convq:
                    convq.pop(0)()
        dram_w = {"w_in": w_in_d, "w_out": w_out_d, "w_cq": w_cq_d, "w_ck": w_ck_d, "w_cv": w_cv_d, "w_co": w_co_d, "w_up": w_up_d, "w_down": w_down_d}

        def bank():
            ctr["b"] = (ctr["b"] + 1) % 8
            return ctr["b"]

        wmin = {"v": 0}

        def wslot():
            ctr["w"] += 1
            if ctr["w"] >= NWORK or ctr["w"] < wmin["v"]:
                ctr["w"] = wmin["v"]
            return ctr["w"]

        def pslot():
            ctr["p"] = (ctr["p"] + 1) % 2
            return ctr["p"]

        def nextw():
            ctr["wb"] = (ctr["wb"] + 1) % 2
            return ctr["wb"]

        def mm(out, lhsT, rhs, start, stop, reads, writes):
            S.op("pe", lambda e: e.matmul(out, lhsT=lhsT, rhs=rhs, start=start, stop=stop), reads=reads, writes=writes)

        def tl(t):
            return slice(t * TT, (t + 1) * TT)

        S.dma("sp", ident[:, :], ident_d, writes=[rconst])
        S.dma("pool", tri[:, :], tri_d, writes=[rconst])
        S.dma("pool", maskb[:, :], mask_d, writes=[rconst])
        S.op("dve", lambda e: e.memset(selb[:, :], 0.0), writes=[rconst])
        S.op("dve", lambda e: e.memset(estrip[:, :], 0.0), writes=[res_])
        S.dma("pool", selb[0:4, :], sel_d, writes=[rconst])
        S.dma("sp", gains[:, :], gains_d, writes=[rconst])
        S.dma("sp", fbn[:, :], fb_d, writes=[rconst])
        S.dma("sp", lrup[:, :], lrup_d, writes=[rconst])
        S.dma("sp", ffnp[:, :], ffnp_d, writes=[rconst])
        S.op("dve", lambda e: e.memset(ones_bf[:, :], 1.0), writes=[rconst])
        S.op("dve", lambda e: e.memset(ones_f[:, :], 1.0), writes=[rconst])
        S.op("dve", lambda e: e.memset(rbf[:, :], 0.0), writes=[rrb])
        S.op("dve", lambda e: e.memset(ones_r0[:, :], 0.0), writes=[rconst])
        S.op("dve", lambda e: e.memset(ones_r0[0:1, :], 1.0), writes=[rconst])
        S.op("dve", lambda e: e.memset(cst[:, 0:1], 1e-6), writes=[rconst])
        S.op("dve", lambda e: e.memset(cst[:, 1:2], 1.0), writes=[rconst])
        S.op("dve", lambda e: e.memset(vaug[:, :, :, :], 0.0), writes=rv)
        S.op("dve", lambda e: e.memset(vaug[:, :, 0, 64:65], 1.0), writes=rv)
        S.op("dve", lambda e: e.memset(vaug[:, :, 1, 0:1], 1.0), writes=rv)
        S.op("dve", lambda e: e.tensor_scalar(out=fbn[:, :], in0=fbn[:, :], scalar1=-1.0, scalar2=None, op0=ALU.mult),
             reads=[rconst], writes=[rconst])
        relb = stage[0:32, 0:4]; relrep = stage[0:32, 8:136]
        xTf = xT[:, :, :].rearrange("p c s -> p (c s)"); hTf = hT[:, :, :].rearrange("p c s -> p (c s)")
        multt = xTf[:, 0:TP]; tf32 = xTf[:, 4096:4096 + TP]; tb16 = hTf[:, 0:TP]
        S.dma("sp", relb, relb_d, writes=[rstage])
        S.dma("sp", work[0:32, 0:5, :], onehot_d.rearrange("b (j n) -> b j n", j=5), writes=rwk)
        S.dma("sp", multt, mult_d, writes=rx[0] + rx[1])
        for h in range(4):
            S.op("dve", lambda e, h=h: e.tensor_copy(out=relrep, in_=stage[0:32, h:h + 1].to_broadcast([32, 128])),
                 reads=[rstage], writes=[rstage])
            for j in range(5):
                b = bank()
                mm(pb[b][:, :], relrep, work[0:32, j, :], True, True, [rstage] + rwk, [rpb[b]])
                S.op("act", lambda e, b=b, j=j: e.activation(out=tf32[:, j * TT:(j + 1) * TT], in_=pb[b][:, :], func=AF.Exp),
                     reads=[rpb[b]], writes=rx[2] + rx[3])
            S.op("dve", lambda e: e.tensor_tensor(out=tb16, in0=tf32, in1=multt, op=ALU.mult),
                 reads=rx[0] + rx[1] + rx[2] + rx[3], writes=rh[0] + rh[1])
            S.dma("sp", tpad_d[h], tb16, reads=rh[0] + rh[1], writes=[rtpad])

        def issue_load(wbi, pieces):
            if pieces[0][0] == "scr_up":
                _, l_, g = pieces[0]
                S.dma("sp", wbuf[wbi][:, 0:WCOLS], wup_s[l_, g], reads=[rs_up[l_][g]], writes=[rw[wbi]])
                return
            if pieces[0][0] == "scr_dn":
                _, l_, m = pieces[0]
                S.dma("sp", wbuf[wbi][:, 0:22 * 128], wdn_s[l_, m], reads=[rs_dn[l_][m]], writes=[rw[wbi]])
                return
            for ((nm, l_, a, b_), c0, nch, ncols, cstride) in pieces:
                src = dram_w[nm][l_][:, a:b_]
                full = wbuf[wbi][:, 0:nch * cstride].rearrange("p (c n) -> p c n", c=nch)[:, :, c0:c0 + ncols]
                S.dma("pool", full, src.rearrange("(c p) n -> p c n", p=128), writes=[rw[wbi]])

        def load_w(wbi, pieces):
            i = ctr["ld"]
            ctr["ld"] += 1
            if plan is None:
                rec.append((wbi, pieces))
                issue_load(wbi, pieces)
                return
            if i == 0:
                issue_load(*plan[0])
            if i + 1 < len(plan):
                issue_load(*plan[i + 1])

        def wsl(wbi, c, cstride, c0, n):
            return wbuf[wbi][:, c * cstride + c0: c * cstride + c0 + n]

        def norm(goff, final_seq=None):
            for c in range(8):
                S.op("act", lambda e, c=c: e.activation(out=hT[:, c, :], in_=xT[:, c, :], func=AF.Square),
                     reads=rx[c], writes=rh[c])
            for t in range(NT):
                b = bank()
                for c in range(8):
                    mm(pb[b][:, :], ones_bf[:, :], hT[:, c, tl(t)], c == 0, c == 7, [rconst, rh[c][t]], [rpb[b]])
                ws = wslot()
                S.op("act", lambda e, b=b, ws=ws: e.activation(out=work[:, ws, :], in_=pb[b][:, :], func=AF.Ln,
                                                               bias=cst[:, 0:1], scale=1.0 / DM),
                     reads=[rpb[b], rconst], writes=[rwk[ws]])
                S.op("act", lambda e, ws=ws: e.activation(out=work[:, ws, :], in_=work[:, ws, :], func=AF.Exp, scale=-0.5),
                     reads=[rwk[ws]], writes=[rwk[ws]])
                for c in range(8):
                    if final_seq is None:
                        S.op("dve", lambda e, c=c, t=t, ws=ws: e.scalar_tensor_tensor(
                            out=hT[:, c, tl(t)], in0=xT[:, c, tl(t)], scalar=gains[:, goff + c:goff + c + 1],
                            in1=work[:, ws, :], op0=ALU.mult, op1=ALU.mult),
                            reads=[rx[c][t], rwk[ws], rconst], writes=[rh[c][t]])
                    else:
                        w2 = wslot()
                        while w2 == ws:
                            w2 = wslot()
                        S.op("dve", lambda e, c=c, t=t, ws=ws, w2=w2: e.scalar_tensor_tensor(
                            out=work[:, w2, :], in0=xT[:, c, tl(t)], scalar=gains[:, goff + c:goff + c + 1],
                            in1=work[:, ws, :], op0=ALU.mult, op1=ALU.mult),
                            reads=[rx[c][t], rwk[ws], rconst], writes=[rwk[w2]])
                        S.dma("sp", yT_d[final_seq, c * 128:(c + 1) * 128, tl(t)], work[:, w2, :], reads=[rwk[w2]])

        def dump(name, src_fn, reads_fn):
            if not dbg:
                return
            for c in range(8):
                for t in range(NT):
                    w2 = wslot()
                    S.op("dve", lambda e, c=c, t=t, w2=w2: e.tensor_copy(out=work[:, w2, :], in_=src_fn(c, t)),
                         reads=[reads_fn(c, t)], writes=[rwk[w2]])
                    S.dma("sp", dbg_d[name][c * 128:(c + 1) * 128, tl(t)], work[:, w2, :], reads=[rwk[w2]])

        def proj_fm(wbi, cstride, c0, M, evac):
            for t in range(NT):
                b = bank()
                for c in range(8):
                    mm(pb[b][0:M, :], wsl(wbi, c, cstride, c0, M), hT[:, c, tl(t)], c == 0, c == 7, [rw[wbi], rh[c][t]], [rpb[b]])
                evac(t, b)

        def proj_v(wbi, cstride, c0):
            for g in range(4):
                b = bank()
                for i in range(4):
                    blk = g * 4 + i
                    for c in range(8):
                        mm(pb[b][:, i * 128:(i + 1) * 128], hT[:, c, blk * 128:(blk + 1) * 128], wsl(wbi, c, cstride, c0, 128),
                           c == 0, c == 7, [rw[wbi], rh[c][g]], [rpb[b]])
                pv = pb[b][:, :].rearrange("p (b n) -> p b n", b=4)
                S.op("dve", lambda e, g=g, pv=pv: e.tensor_copy(out=vaug[:, g * 4:(g + 1) * 4, 0, 0:64], in_=pv[:, :, 0:64]),
                     reads=[rpb[b]], writes=[rv[g]])
                S.op("act", lambda e, g=g, pv=pv: e.copy(out=vaug[:, g * 4:(g + 1) * 4, 1, 64:128], in_=pv[:, :, 64:128]),
                     reads=[rpb[b]], writes=[rv[g]])

        def evac_q(t, b):
            S.op("act", lambda e: e.activation(out=qT[:, tl(t)], in_=pb[b][:, :], func=AF.Copy, scale=0.125),
                 reads=[rpb[b]], writes=[rq[t]])

        def evac_k(t, b):
            S.op("dve", lambda e: e.tensor_copy(out=kT[:, tl(t)], in_=pb[b][:, :]), reads=[rpb[b]], writes=[rk[t]])

        def finalize(hh, ci, t, ob, normalize, mb=None):
            rows = slice(0, 64) if hh == 0 else slice(64, 128)
            drow = 64 if hh == 0 else 0
            if not normalize:
                S.op("act", lambda e: e.copy(out=mixedT[rows, ci, tl(t)], in_=pb[ob][rows, :]), reads=[rpb[ob]], writes=[rm[ci][t]])
                return
            w1 = wslot()
            S.op("act", lambda e: e.activation(out=work[drow:drow + 1, w1, :], in_=pb[ob][drow:drow + 1, :], func=AF.Ln),
                 reads=[rpb[ob]], writes=[rwk[w1]])
            S.op("act", lambda e: e.activation(out=work[drow:drow + 1, w1, :], in_=work[drow:drow + 1, w1, :], func=AF.Exp, scale=-1.0),
                 reads=[rwk[w1]], writes=[rwk[w1]])
            if mb is None:
                mb = MB[0]
            ncol = 64 if hh == 0 else 128
            w3 = wslot()
            hl = work[:, w3, :].bitcast(BF16)
            S.op("dve", lambda e: e.tensor_copy(out=hl[drow:drow + 1, 0:TT], in_=work[drow:drow + 1, w1, :]),
                 reads=[rwk[w1]], writes=[rwk[w3]])
            S.op("dve", lambda e: e.tensor_tensor(out=hl[drow:drow + 1, TT:2 * TT], in0=work[drow:drow + 1, w1, :],
                                                  in1=hl[drow:drow + 1, 0:TT], op=ALU.subtract),
                 reads=[rwk[w1], rwk[w3]], writes=[rwk[w3]])
            mm(pb[mb][0:ncol, :], ones_bf[drow:drow + 1, 0:ncol], hl[drow:drow + 1, 0:TT], True, False, [rconst, rwk[w3]], [rpb[mb]])
            mm(pb[mb][0:ncol, :], ones_bf[drow:drow + 1, 0:ncol], hl[drow:drow + 1, TT:2 * TT], False, True, [rconst, rwk[w3]], [rpb[mb]])
            w2 = wslot()
            S.op("act", lambda e: e.copy(out=work[rows, w2, :], in_=pb[mb][rows, :]), reads=[rpb[mb]], writes=[rwk[w2]])
            S.op("dve", lambda e: e.tensor_tensor(out=mixedT[rows, ci, tl(t)], in0=pb[ob][rows, :], in1=work[rows, w2, :], op=ALU.mult),
                 reads=[rpb[ob], rwk[w2]], writes=[rm[ci][t]])

        OB = [0, 1]; ZB = [2, 3]; CB = [4, 5, 6]; MB = [7]; ZB4 = [2, 3, 4, 5, 6]
        rot = {"ob": 0, "zb": 0, "cb": 0, "pl": 0, "pa": 0, "pp": 0, "pg": 0, "zb4": 0}

        def take(key, pool):
            rot[key] = (rot[key] + 1) % len(pool)
            return pool[rot[key]]
        DEPTH = 2

        def attn(kind, hh, ci, head=0, ctr_res=None):
            base = hh * 64
            for t in range(NT):
                ob = take("ob", OB)
                if kind == "cross":
                    kbs = [0, 1]
                elif kind == "sb":
                    kbs = list(range(4 * (t + 1) - 1, -1, -1))
                else:
                    kbs = list(range(4 * (t + 1)))
                nk = len(kbs)

                def stZ(idx, kb):
                    first = idx == 0
                    last = idx == nk - 1
                    diag = (kind != "cross") and kb >= 4 * t
                    o = kb * 128 - t * TT
                    zb = take("zb4", ZB4)
                    if kind == "cross":
                        kcol = slice(ci * 256 + kb * 128, ci * 256 + (kb + 1) * 128)
                        vl = vaug[:, kb * 2 + ci, hh, :]
                        rkk, rvv = rk[0], rv[0]
                    else:
                        kcol = slice(kb * 128, (kb + 1) * 128)
                        vl = vaug[:, kb, hh, :]
                        rkk, rvv = rk[kb // 4], rv[kb // 4]
                    st = {"first": first, "last": last, "diag": diag, "o": o, "vl": vl, "rvv": rvv, "zb": zb, "kb": kb}
                    mm(pb[zb][:, :], kT[base:base + 64, kcol], qT[base:base + 64, tl(t)], True, kind != "fox",
                       [rkk, rq[t]], [rpb[zb]])
                    if kind == "fox":
                        mm(pb[zb][:, :], selb[:, head * 128:(head + 1) * 128], estrip[:, tl(t)], False, True,
                           [rconst, res_], [rpb[zb]])
                    return st

                def stA(st, sbmid=None):
                    first, last, diag, o, zb, kb = st["first"], st["last"], st["diag"], st["o"], st["zb"], st["kb"]
                    if kind == "sb":
                        wsp = wslot()
                        S.op("act", lambda e: e.activation(out=work[:, wsp, :], in_=pb[zb][:, :], func=AF.Exp),
                             reads=[rpb[zb]], writes=[rwk[wsp]])
                        if sbmid is not None:
                            sbmid()
                        S.op("act", lambda e: e.activation(out=work[:, wsp, :], in_=work[:, wsp, :], func=AF.Ln,
                                                           bias=cst[:, 1:2], scale=1.0),
                             reads=[rwk[wsp], rconst], writes=[rwk[wsp]])
                        ls = take("pl", [0, 1]); Lb = pbf[:, ls, :]
                        st["ls"] = ls
                        if diag:
                            S.op("dve", lambda e: e.scalar_tensor_tensor(
                                out=Lb, in0=work[:, wsp, :], scalar=-1.0, in1=maskb[:, 384 - o:384 - o + TT], op0=ALU.mult, op1=ALU.mult),
                                reads=[rwk[wsp], rconst], writes=[rpbf[ls]])
                        else:
                            S.op("dve", lambda e: e.tensor_scalar(out=Lb, in0=work[:, wsp, :], scalar1=-1.0, scalar2=None,
                                                                  op0=ALU.mult), reads=[rwk[wsp]], writes=[rpbf[ls]])
                    else:
                        ps_ = take("pp", [0, 1, 2])
                        P = pbf[:, ps_, :]
                        st["ps"] = ps_
                        if kind == "fox":
                            if diag:
                                wa = wslot()
                                S.op("dve", lambda e: e.tensor_scalar(
                                    out=work[:, wa, :], in0=pb[zb][:, :], scalar1=ckneg[:, kb * 4 + head:kb * 4 + head + 1], scalar2=80.0,
                                    op0=ALU.add, op1=ALU.min), reads=[rpb[zb], rck], writes=[rwk[wa]])
                                S.op("act", lambda e: e.activation(out=P, in_=work[:, wa, :], func=AF.Exp),
                                     reads=[rwk[wa]], writes=[rpbf[ps_]])
                                S.op("dve", lambda e: e.tensor_tensor(out=P, in0=P, in1=maskb[:, 385 - o:385 - o + TT], op=ALU.mult),
                                     reads=[rpbf[ps_], rconst], writes=[rpbf[ps_]])
                            else:
                                S.op("act", lambda e: e.activation(out=P, in_=pb[zb][:, :], func=AF.Exp,
                                                                   bias=ckneg[:, kb * 4 + head:kb * 4 + head + 1], scale=1.0),
                                     reads=[rpb[zb], rck], writes=[rpbf[ps_]])
                        else:
                            S.op("act", lambda e: e.activation(out=P, in_=pb[zb][:, :], func=AF.Exp),
                                 reads=[rpb[zb]], writes=[rpbf[ps_]])
                            if kind == "dil":
                                d = -o
                                S.op("dve", lambda e: e.tensor_tensor(out=P, in0=P, in1=estrip[:, d + 384:d + 384 + TT], op=ALU.mult),
                                     reads=[rpbf[ps_], res_], writes=[rpbf[ps_]])
                    return st

                def stB(st):
                    if kind != "sb":
                        return
                    first, last, ls = st["first"], st["last"], st["ls"]
                    Lb = pbf[:, ls, :]
                    cb = st["zb"]
                    st["cb"] = cb
                    rb_ = MB[0]
                    if not last:
                        mm(pb[rb_][:, :], ones_bf[:, :], Lb, first, True, [rconst, rpbf[ls]], [rpb[rb_]])
                    mm(pb[cb][:, :], tri[:, :], Lb, False, first, [rconst, rpbf[ls]], [rpb[cb]])
                    if not first:
                        mm(pb[cb][:, :], ones_r0[:, :], rbf[:, :], False, True, [rconst, rrb], [rpb[cb]])
                    if not last:
                        S.op("dve", lambda e: e.tensor_copy(out=rbf[0:1, :], in_=pb[rb_][0:1, :]), reads=[rpb[rb_]], writes=[rrb])

                def stC1(st):
                    diag, o = st["diag"], st["o"]
                    if kind == "sb":
                        cb = st["cb"]
                        ps_ = take("pa", [4, 5])
                        st["ps"] = ps_
                        P = pbf[:, ps_, :]
                        S.op("act", lambda e: e.activation(out=P, in_=pb[cb][:, :], func=AF.Exp),
                             reads=[rpb[cb]], writes=[rpbf[ps_]])
                        if diag:
                            S.op("dve", lambda e: e.tensor_tensor(out=P, in0=P, in1=maskb[:, 384 - o:384 - o + TT], op=ALU.mult),
                                 reads=[rpbf[ps_], rconst], writes=[rpbf[ps_]])

                def stC2(st):
                    first, last, vl, rvv = st["first"], st["last"], st["vl"], st["rvv"]
                    ps_ = st["ps"]
                    mm(pb[ob][:, :], vl, pbf[:, ps_, :], first, last, [rvv, rpbf[ps_]], [rpb[ob]])

                sts = []
                for n in range(-1, nk + 2):
                    if 0 <= n + 1 < nk:
                        sts.append(stZ(n + 1, kbs[n + 1]))
                    c1 = (lambda n=n: stC1(sts[n - 2])) if 0 <= n - 2 < nk else None
                    if 0 <= n < nk:
                        stA(sts[n], c1 if kind == "sb" else None)
                        if kind != "sb" and c1 is not None:
                            c1()
                    elif c1 is not None:
                        c1()
                    if 0 <= n - 1 < nk:
                        stB(sts[n - 1])
                    if 0 <= n - 2 < nk:
                        stC2(sts[n - 2])
                finalize(hh, ci, t, ob, kind != "sb")

        def attn_sb(ci):
            items = []
            seg = 0
            for hh in range(2):
                for t in range(NT):
                    kbs = list(range(4 * (t + 1) - 1, -1, -1))
                    for idx, kb in enumerate(kbs):
                        items.append({"hh": hh, "t": t, "idx": idx, "kb": kb, "first": idx == 0, "last": idx == len(kbs) - 1,
                                      "diag": kb >= 4 * t, "o": kb * 128 - t * TT, "ob": OB[seg % 2]})
                    seg += 1
            n_it = len(items)

            def stZ(st):
                hh, t, kb = st["hh"], st["t"], st["kb"]
                base = hh * 64
                zb = take("zb4", ZB4)
                st["zb"] = zb
                mm(pb[zb][:, :], kT[base:base + 64, kb * 128:(kb + 1) * 128], qT[base:base + 64, tl(t)], True, True,
                   [rk[kb // 4], rq[t]], [rpb[zb]])

            def stA(st, mid):
                zb, diag, o = st["zb"], st["diag"], st["o"]
                wsp = wslot()
                S.op("act", lambda e: e.activation(out=work[:, wsp, :], in_=pb[zb][:, :], func=AF.Exp),
                     reads=[rpb[zb]], writes=[rwk[wsp]])
                if mid is not None:
                    mid()
                S.op("act", lambda e: e.activation(out=work[:, wsp, :], in_=work[:, wsp, :], func=AF.Ln, bias=cst[:, 1:2], scale=1.0),
                     reads=[rwk[wsp], rconst], writes=[rwk[wsp]])
                ls = take("pl", [0, 1]); Lb = pbf[:, ls, :]
                st["ls"] = ls
                if diag:
                    S.op("dve", lambda e: e.scalar_tensor_tensor(
                        out=Lb, in0=work[:, wsp, :], scalar=-1.0, in1=maskb[:, 384 - o:384 - o + TT], op0=ALU.mult, op1=ALU.mult),
                        reads=[rwk[wsp], rconst], writes=[rpbf[ls]])
                else:
                    S.op("dve", lambda e: e.tensor_scalar(out=Lb, in0=work[:, wsp, :], scalar1=-1.0, scalar2=None, op0=ALU.mult),
                         reads=[rwk[wsp]], writes=[rpbf[ls]])

            def stB(st):
                first, last, ls, cb = st["first"], st["last"], st["ls"], st["zb"]
                Lb = pbf[:, ls, :]
                rb_ = MB[0]
                if not last:
                    mm(pb[rb_][:, :], ones_bf[:, :], Lb, first, True, [rconst, rpbf[ls]], [rpb[rb_]])
                mm(pb[cb][:, :], tri[:, :], Lb, False, first, [rconst, rpbf[ls]], [rpb[cb]])
                if not first:
                    mm(pb[cb][:, :], ones_r0[:, :], rbf[:, :], False, True, [rconst, rrb], [rpb[cb]])
                if not last:
                    S.op("dve", lambda e: e.tensor_copy(out=rbf[0:1, :], in_=pb[rb_][0:1, :]), reads=[rpb[rb_]], writes=[rrb])

            def stC1(st):
                diag, o, cb = st["diag"], st["o"], st["zb"]
                ps_ = take("pa", [4, 5])
                st["ps"] = ps_
                P = pbf[:, ps_, :]
                S.op("act", lambda e: e.activation(out=P, in_=pb[cb][:, :], func=AF.Exp), reads=[rpb[cb]], writes=[rpbf[ps_]])
                if diag:
                    S.op("dve", lambda e: e.tensor_tensor(out=P, in0=P, in1=maskb[:, 384 - o:384 - o + TT], op=ALU.mult),
                         reads=[rpbf[ps_], rconst], writes=[rpbf[ps_]])

            def stC2(st):
                hh, t, kb, ob, ps_ = st["hh"], st["t"], st["kb"], st["ob"], st["ps"]
                mm(pb[ob][:, :], vaug[:, kb, hh, :], pbf[:, ps_, :], st["first"], st["last"], [rv[kb // 4], rpbf[ps_]], [rpb[ob]])
                if st["last"]:
                    finalize(hh, ci, t, ob, False)

            for n in range(-1, n_it + 2):
                if 0 <= n + 1 < n_it:
                    stZ(items[n + 1])
                c1 = (lambda n=n: stC1(items[n - 2])) if 0 <= n - 2 < n_it else None
                if 0 <= n < n_it:
                    stA(items[n], c1)
                elif c1 is not None:
                    c1()
                if 0 <= n - 1 < n_it:
                    stB(items[n - 1])
                if 0 <= n - 2 < n_it:
                    stC2(items[n - 2])

        estripB = work[:, 0:3, :].rearrange("p a s -> p (a s)").bitcast(BF16)[:, 0:EW]

        def attn2(kind, ci, head0, ctr_res=None):
            HD = []
            for hh in range(2):
                HD.append({"hh": hh, "base": hh * 64, "head": head0 + hh, "ob": hh, "zp": [2, 3, 4] if hh == 0 else [5, 6, 7],
                           "pp": [0, 1, 2] if hh == 0 else [3, 4, 5], "zi": 0, "pi": 0,
                           "strip": (estrip, [res_]) if hh == 0 else (estripB, rwk[0:3])})
            for t in range(NT):
                kbs = [0, 1] if kind == "cross" else list(range(4 * (t + 1)))
                nk = len(kbs)

                def stZ1(H, idx, kb):
                    hh, base = H["hh"], H["base"]
                    H["zi"] = (H["zi"] + 1) % 3
                    zb = H["zp"][H["zi"]]
                    if kind == "cross":
                        kcol = slice(ci * 256 + kb * 128, ci * 256 + (kb + 1) * 128)
                        vl = vaug[:, kb * 2 + ci, hh, :]
                        rkk, rvv = rk[0], rv[0]
                    else:
                        kcol = slice(kb * 128, (kb + 1) * 128)
                        vl = vaug[:, kb, hh, :]
                        rkk, rvv = rk[kb // 4], rv[kb // 4]
                    st = {"first": idx == 0, "last": idx == nk - 1, "diag": (kind != "cross") and kb >= 4 * t,
                          "o": kb * 128 - t * TT, "vl": vl, "rvv": rvv, "zb": zb, "kb": kb}
                    mm(pb[zb][:, :], kT[base:base + 64, kcol], qT[base:base + 64, tl(t)], True, kind != "fox",
                       [rkk, rq[t]], [rpb[zb]])
                    return st

                def stZ2(H, st):
                    if kind == "fox":
                        zb, head = st["zb"], H["head"]
                        mm(pb[zb][:, :], selb[:, head * 128:(head + 1) * 128], estrip[:, tl(t)], False, True,
                           [rconst, res_], [rpb[zb]])

                def stA(H, st):
                    diag, o, zb, kb, head = st["diag"], st["o"], st["zb"], st["kb"], H["head"]
                    H["pi"] = (H["pi"] + 1) % 3
                    ps_ = H["pp"][H["pi"]]
                    P = pbf[:, ps_, :]
                    st["ps"] = ps_
                    if kind == "fox":
                        if diag:
                            wa = wslot()
                            S.op("dve", lambda e: e.tensor_scalar(
                                out=work[:, wa, :], in0=pb[zb][:, :], scalar1=ckneg[:, kb * 4 + head:kb * 4 + head + 1], scalar2=80.0,
                                op0=ALU.add, op1=ALU.min), reads=[rpb[zb], rck], writes=[rwk[wa]])
                            S.op("act", lambda e: e.activation(out=P, in_=work[:, wa, :], func=AF.Exp),
                                 reads=[rwk[wa]], writes=[rpbf[ps_]])
                            st["mask"] = lambda: S.op("dve", lambda e: e.tensor_tensor(out=P, in0=P, in1=maskb[:, 385 - o:385 - o + TT], op=ALU.mult),
                                                      reads=[rpbf[ps_], rconst], writes=[rpbf[ps_]])
                        else:
                            S.op("act", lambda e: e.activation(out=P, in_=pb[zb][:, :], func=AF.Exp,
                                                               bias=ckneg[:, kb * 4 + head:kb * 4 + head + 1], scale=1.0),
                                 reads=[rpb[zb], rck], writes=[rpbf[ps_]])
                    else:
                        S.op("act", lambda e: e.activation(out=P, in_=pb[zb][:, :], func=AF.Exp),
                             reads=[rpb[zb]], writes=[rpbf[ps_]])
                        if kind == "dil":
                            d = -o
                            strip, rstrip = H["strip"]
                            S.op("dve", lambda e: e.tensor_tensor(out=P, in0=P, in1=strip[:, d + 384:d + 384 + TT], op=ALU.mult),
                                 reads=[rpbf[ps_]] + rstrip, writes=[rpbf[ps_]])

                def stC(H, st):
                    ps_ = st["ps"]
                    ob = H["ob"]
                    mm(pb[ob][:, :], st["vl"], pbf[:, ps_, :], st["first"], st["last"], [st["rvv"], rpbf[ps_]], [rpb[ob]])

                sts = [[], []]
                for n in range(-1, nk + 2):
                    if 0 <= n + 1 < nk:
                        for H in HD:
                            sts[H["hh"]].append(stZ1(H, n + 1, kbs[n + 1]))
                        for H in HD:
                            stZ2(H, sts[H["hh"]][n + 1])
                    if 0 <= n - 1 < nk:
                        for H in HD:
                            mk_ = sts[H["hh"]][n - 1].get("mask")
                            if mk_ is not None:
                                mk_()
                    if 0 <= n < nk:
                        for H in HD:
                            stA(H, sts[H["hh"]][n])
                    if 0 <= n - 2 < nk:
                        for H in HD:
                            stC(H, sts[H["hh"]][n - 2])
                for H in HD:
                    finalize(H["hh"], ci, t, H["ob"], True, mb=H["zp"][0])

        cT = mixedT[0:4, 6:8, :].rearrange("p a s -> p (a s)").bitcast(F32)
        lsf = work[0:4, 0:4, :].rearrange("p a s -> p (a s)")

        def w_in_piece(l, col0, n):
            return ("w_in", l, col0, col0 + n)

        def attn_mixer(l, kind, off, mi):
            for hp in range(2):
                wbi = nextw()
                pieces = [(w_in_piece(l, off + hp * 128, 128), 0, 8, 128, 512),
                          (w_in_piece(l, off + 256 + hp * 128, 128), 128, 8, 128, 512),
                          (w_in_piece(l, off + 512 + hp * 128, 128), 256, 8, 128, 512)]
                if kind == "fox" and hp == 0:
                    pieces.append((w_in_piece(l, OFF_F, 4), 384, 8, 4, 512))
                load_w(wbi, pieces)
                drip(4)
                proj_fm(wbi, 512, 0, 128, evac_q)
                proj_fm(wbi, 512, 128, 128, evac_k)
                proj_v(wbi, 512, 256)
                ctr_res = None
                if kind == "fox":
                    ctr_res = [[rm[6][t], rm[7][t]] for t in range(NT)]
                if kind == "fox" and hp == 0:
                    def evac_f(t, b):
                        S.op("act", lambda e: e.activation(out=lsf[:, tl(t)], in_=pb[b][0:4, :], func=AF.Exp, bias=fbn[0:4, l:l + 1], scale=-1.0),
                             reads=[rpb[b], rconst], writes=[rwk[t]])
                        S.op("act", lambda e: e.activation(out=lsf[:, tl(t)], in_=lsf[:, tl(t)], func=AF.Ln, bias=cst[0:4, 1:2], scale=1.0),
                             reads=[rwk[t], rconst], writes=[rwk[t]])
                        S.op("dve", lambda e: e.tensor_scalar(out=lsf[:, tl(t)], in0=lsf[:, tl(t)], scalar1=-0.5, scalar2=None, op0=ALU.mult),
                             reads=[rwk[t]], writes=[rwk[t]])
                    proj_fm(wbi, 512, 384, 4, evac_f)
                    allc = [r for t in range(NT) for r in ctr_res[t]]
                    S.op("dve", lambda e: e.tensor_tensor_scan(out=cT, data0=lsf, data1=lsf, initial=0.0, op0=ALU.add, op1=ALU.add),
                         reads=rwk[0:4], writes=allc)
                    S.op("dve", lambda e: e.tensor_copy(out=estrip[0:4, 0:SEQ], in_=cT), reads=allc, writes=[res_])
                    b = bank()
                    for blk in range(16):
                        S.op("pe", lambda e, blk=blk, b=b: e.transpose(out=pb[b][:, blk * 4:(blk + 1) * 4], in_=cT[0:4, blk * 128:(blk + 1) * 128],
                                                                     identity=ident[0:4, 0:4]),
                             reads=[rconst] + ctr_res[blk // 4], writes=[rpb[b]])
                    S.op("dve", lambda e, b=b: e.tensor_scalar(out=ckneg[:, :], in0=pb[b][:, 0:64], scalar1=-1.0, scalar2=None, op0=ALU.mult),
                         reads=[rpb[b]], writes=[rck])
                if kind == "sb":
                    attn_sb(mi * 2 + hp)
                else:
                    if kind == "dil":
                        wmin["v"] = 3
                        S.dma("sp", estrip[:, :], bass.AP(tpad_d.tensor, (hp * 2) * 128 * TP + 127, [[TP - 1, 128], [1, EW]]),
                              reads=[rtpad], writes=[res_])
                        S.dma("sp", estripB, bass.AP(tpad_d.tensor, (hp * 2 + 1) * 128 * TP + 127, [[TP - 1, 128], [1, EW]]),
                              reads=[rtpad], writes=rwk[0:3])
                    attn2(kind, mi * 2 + hp, hp * 2, ctr_res=ctr_res)
                    wmin["v"] = 0

        def lru(l):
            wbi = nextw()
            load_w(wbi, [(w_in_piece(l, OFF_LX, 512), 0, 8, 512, 512)])
            S.dma("sp", wabd[:, :, :], wabd_d[l].rearrange("c p n -> p c n"), writes=[rwab])
            S.dma("sp", wxbd[:, :, :], wxbd_d[l].rearrange("c p n -> p c n"), writes=[rwab])
            lp = lambda c, k: lrup[:, (l * 2 + c) * 8 + k:(l * 2 + c) * 8 + k + 1]
            for c in range(2):
                S.op("act", lambda e, c=c: e.activation(out=coef[:, c:c + 1], in_=lp(c, 7), func=AF.Exp, scale=-1.0), reads=[rconst], writes=[rcoef])
                S.op("act", lambda e, c=c: e.activation(out=coef[:, c:c + 1], in_=coef[:, c:c + 1], func=AF.Ln, bias=cst[:, 1:2], scale=1.0),
                     reads=[rcoef, rconst], writes=[rcoef])
                S.op("dve", lambda e, c=c: e.tensor_scalar(out=coef[:, c:c + 1], in0=coef[:, c:c + 1], scalar1=-8.0, scalar2=None, op0=ALU.mult),
                     reads=[rcoef], writes=[rcoef])
            raws = [(xr, rxr), (gr, rgr)]

            def unit_steps(c, t):
                raw, rraw = raws[c]
                wA, wB, wC = 3 * c, 3 * c + 1, 3 * c + 2
                pR, pI, pG = 3 * c, 3 * c + 1, 3 * c + 2
                stt = {}
                steps = []

                def s_mm():
                    xb_ = bank()
                    for cc in range(8):
                        mm(pb[xb_][:, :], wsl(wbi, cc, 512, c * 128, 128), hT[:, cc, tl(t)], cc == 0, cc == 7, [rw[wbi], rh[cc][t]], [rpb[xb_]])
                    gb_ = bank()
                    for cc in range(8):
                        mm(pb[gb_][:, :], wsl(wbi, cc, 512, 256 + c * 128, 128), hT[:, cc, tl(t)], cc == 0, cc == 7, [rw[wbi], rh[cc][t]], [rpb[gb_]])
                    stt["xb"] = xb_; stt["gb"] = gb_
                steps.append(s_mm)

                def s_evac():
                    if t > 0:
                        S.op("dve", lambda e: e.tensor_copy(out=raw[:, 0:3], in_=raw[:, 512:515]), reads=rraw, writes=rraw)
                    else:
                        S.op("dve", lambda e: e.memset(raw[:, 0:3], 0.0), writes=rraw)
                    xb_, gb_ = stt["xb"], stt["gb"]
                    S.op("act", lambda e: e.copy(out=raw[:, 3:515], in_=pb[xb_][:, :]), reads=[rpb[xb_]], writes=rraw)
                    S.op("act", lambda e: e.activation(out=pbf[:, pG, :], in_=pb[gb_][:, :], func=AF.Gelu), reads=[rpb[gb_]], writes=[rpbf[pG]])
                steps.append(s_evac)

                def s_conv():
                    S.op("dve", lambda e: e.tensor_scalar(out=work[:, wA, :], in0=raw[:, 3:515], scalar1=lp(c, 3), scalar2=lp(c, 4),
                                                          op0=ALU.mult, op1=ALU.add), reads=rraw + [rconst], writes=[rwk[wA]])
                    for j in range(3):
                        S.op("dve", lambda e, j=j: e.scalar_tensor_tensor(out=work[:, wA, :], in0=raw[:, j:j + TT], scalar=lp(c, j),
                                                                         in1=work[:, wA, :], op0=ALU.mult, op1=ALU.add),
                             reads=rraw + [rconst, rwk[wA]], writes=[rwk[wA]])
                steps.append(s_conv)

                def s_gmm():
                    rb_ = bank()
                    mm(pb[rb_][:, :], wabd[:, c, :], work[:, wA, :], True, True, [rwab, rwk[wA]], [rpb[rb_]])
                    ib_ = bank()
                    mm(pb[ib_][:, :], wxbd[:, c, :], work[:, wA, :], True, True, [rwab, rwk[wA]], [rpb[ib_]])
                    stt["rb"] = rb_; stt["ib"] = ib_
                steps.append(s_gmm)

                def s_sig():
                    rb_, ib_ = stt["rb"], stt["ib"]
                    S.op("act", lambda e: e.activation(out=pbf[:, pR, :], in_=pb[rb_][:, :], func=AF.Sigmoid, bias=lp(c, 5), scale=1.0),
                         reads=[rpb[rb_], rconst], writes=[rpbf[pR]])
                    S.op("act", lambda e: e.activation(out=pbf[:, pI, :], in_=pb[ib_][:, :], func=AF.Sigmoid, bias=lp(c, 6), scale=1.0),
                         reads=[rpb[ib_], rconst], writes=[rpbf[pI]])
                steps.append(s_sig)

                def s_a():
                    S.op("act", lambda e: e.activation(out=work[:, wB, :], in_=pbf[:, pR, :], func=AF.Exp, scale=coef[:, c:c + 1]),
                         reads=[rpbf[pR], rcoef], writes=[rwk[wB]])
                    S.op("dve", lambda e: e.tensor_tensor(out=work[:, wC, :], in0=work[:, wB, :], in1=work[:, wB, :], op=ALU.mult),
                         reads=[rwk[wB]], writes=[rwk[wC]])
                    S.op("dve", lambda e: e.tensor_scalar(out=work[:, wC, :], in0=work[:, wC, :], scalar1=1.0, scalar2=0.0,
                                                          op0=ALU.subtract, op1=ALU.min), reads=[rwk[wC]], writes=[rwk[wC]])
                steps.append(s_a)

                def s_sqrt():
                    S.op("act", lambda e: e.activation(out=work[:, wC, :], in_=work[:, wC, :], func=AF.Sqrt, scale=-1.0),
                         reads=[rwk[wC]], writes=[rwk[wC]])
                    S.op("dve", lambda e: e.tensor_tensor(out=work[:, wC, :], in0=work[:, wC, :], in1=pbf[:, pI, :], op=ALU.mult),
                         reads=[rwk[wC], rpbf[pI]], writes=[rwk[wC]])
                    S.op("dve", lambda e: e.tensor_tensor(out=work[:, wC, :], in0=work[:, wC, :], in1=work[:, wA, :], op=ALU.mult),
                         reads=[rwk[wC], rwk[wA]], writes=[rwk[wC]])
                steps.append(s_sqrt)

                def s_scan():
                    S.op("dve", lambda e: e.tensor_tensor_scan(
                        out=work[:, wA, :], data0=work[:, wB, :], data1=work[:, wC, :],
                        initial=(0.0 if t == 0 else hlast[:, c:c + 1]), op0=ALU.mult, op1=ALU.add),
                        reads=[rwk[wB], rwk[wC], rhl[c]], writes=[rwk[wA]])
                    S.op("dve", lambda e: e.tensor_copy(out=hlast[:, c:c + 1], in_=work[:, wA, TT - 1:TT]),
                         reads=[rwk[wA]], writes=[rhl[c]])
                    S.op("dve", lambda e: e.tensor_tensor(out=mixedT[:, 6 + c, tl(t)], in0=work[:, wA, :], in1=pbf[:, pG, :],
                                                          op=ALU.mult), reads=[rwk[wA], rpbf[pG]], writes=[rm[6 + c][t]])
                steps.append(s_scan)
                return steps

            allsteps = [(unit_steps(0, t), unit_steps(1, t)) for t in range(NT)]
            allsteps[0][0][0](); allsteps[0][1][0]()
            for t in range(NT):
                s0, s1 = allsteps[t]
                for i in range(1, len(s0)):
                    if i == 3 and t + 1 < NT:
                        allsteps[t + 1][0][0](); allsteps[t + 1][1][0]()
                    s0[i]()
                    s1[i]()

        def out_proj(src_d, nk, l_):
            for half in range(2):
                wbi = nextw()
                load_w(wbi, [((src_d, l_, half * 512, (half + 1) * 512), 0, nk, 512, 512)])
                for m4 in range(4):
                    m = half * 4 + m4
                    for t in range(NT):
                        b = bank()
                        for c in range(nk):
                            mm(pb[b][:, :], wsl(wbi, c, 512, m4 * 128, 128), mixedT[:, c, tl(t)], c == 0, c == nk - 1,
                               [rw[wbi], rm[c][t]], [rpb[b]])
                        S.op("dve", lambda e, b=b, m=m, t=t: e.tensor_tensor(out=xT[:, m, tl(t)], in0=pb[b][:, :], in1=xT[:, m, tl(t)], op=ALU.add),
                             reads=[rpb[b], rx[m][t]], writes=[rx[m][t]])

        def cross(l, s):
            rstdm = pbf[:, 5, :].bitcast(F32)
            memT = work[:, 0:4, :].rearrange("p a s -> p (a s)").rearrange("p (c n) -> p c n", c=8)
            S.dma("sp", memT, memT_d[s].rearrange("(c p) n -> p c n", p=128), writes=rwk[0:4])
            S.op("act", lambda e: e.activation(out=memn[:, :, :], in_=memT, func=AF.Square), reads=rwk[0:4], writes=[rmemn])
            b = bank()
            for c in range(8):
                mm(pb[b][:, 0:NMEM], ones_bf[:, :], memn[:, c, :], c == 0, c == 7, [rconst, rmemn], [rpb[b]])
            S.op("act", lambda e, b=b: e.activation(out=rstdm, in_=pb[b][:, 0:NMEM], func=AF.Ln, bias=cst[:, 0:1], scale=1.0 / DM),
                 reads=[rpb[b], rconst], writes=[rpbf[5]])
            S.op("act", lambda e: e.activation(out=rstdm, in_=rstdm, func=AF.Exp, scale=-0.5), reads=[rpbf[5]], writes=[rpbf[5]])
            goff = (l * 4 + G_MEM) * 8
            for c in range(8):
                S.op("dve", lambda e, c=c: e.scalar_tensor_tensor(out=memn[:, c, :], in0=memT[:, c, :], scalar=gains[:, goff + c:goff + c + 1],
                                                                  in1=rstdm, op0=ALU.mult, op1=ALU.mult),
                     reads=rwk[0:4] + [rpbf[5], rconst], writes=[rmemn])
            wmin["v"] = 4
            norm((l * 4 + G_CROSS) * 8)
            wmin["v"] = 0
            wbi = nextw()
            load_w(wbi, [(("w_ck", l, 0, 256), 0, 8, 256, 768), (("w_cv", l, 0, 256), 256, 8, 256, 768), (("w_cq", l, 0, 256), 512, 8, 256, 768)])
            for j in range(2):
                b = bank()
                for c in range(8):
                    mm(pb[b][:, 0:NMEM], wsl(wbi, c, 768, j * 128, 128), memn[:, c, :], c == 0, c == 7, [rw[wbi], rmemn], [rpb[b]])
                S.op("dve", lambda e, b=b, j=j: e.tensor_copy(out=kT[:, j * 256:(j + 1) * 256], in_=pb[b][:, 0:NMEM]), reads=[rpb[b]], writes=[rk[0]])
            for mb in range(2):
                b = bank()
                for c in range(8):
                    mm(pb[b][:, 0:256], memn[:, c, mb * 128:(mb + 1) * 128], wsl(wbi, c, 768, 256, 256), c == 0, c == 7, [rw[wbi], rmemn], [rpb[b]])
                pv = pb[b][:, 0:256].rearrange("p (b n) -> p b n", b=2)
                S.op("dve", lambda e, mb=mb, pv=pv: e.tensor_copy(out=vaug[:, mb * 2:mb * 2 + 2, 0, 0:64], in_=pv[:, :, 0:64]),
                     reads=[rpb[b]], writes=[rv[0]])
                S.op("act", lambda e, mb=mb, pv=pv: e.copy(out=vaug[:, mb * 2:mb * 2 + 2, 1, 64:128], in_=pv[:, :, 64:128]),
                     reads=[rpb[b]], writes=[rv[0]])
            for hp in range(2):
                proj_fm(wbi, 768, 512 + hp * 128, 128, evac_q)
                attn2("cross", hp, hp * 2)
            out_proj("w_co", 2, l)

        def ffn(l):
            norm((l * 4 + G_FFN) * 8)

            def actv(j):
                return mixedT[:, j // 4, tl(j % 4)]

            def ract(j):
                return rm[j // 4][j % 4]
            fp = lambda j, k: ffnp[:, (l * 44 + j) * 4 + k:(l * 44 + j) * 4 + k + 1]
            uraw = [(qT[:, 0:1032].bitcast(F32), rq), (memn[:, :, :].rearrange("p c n -> p (c n)")[:, 0:1032].bitcast(F32), [rmemn])]
            graw = [(kT[:, 0:1032].bitcast(F32), rk), (pbf[:, :, :].rearrange("p a s -> p (a s)")[:, 0:1032].bitcast(F32), rpbf)]
            groups = [(g * 3, min(3, 22 - g * 3)) for g in range(8)]
            par = {"i": 0}

            def front(t, wbi, jj, j):
                par["i"] ^= 1
                p = par["i"]
                ub = bank()
                for c in range(8):
                    mm(pb[ub][:, :], wsl(wbi, c, 768, jj * 128, 128), hT[:, c, tl(t)], c == 0, c == 7, [rw[wbi], rh[c][t]], [rpb[ub]])
                gb_ = bank()
                for c in range(8):
                    mm(pb[gb_][:, :], wsl(wbi, c, 768, 384 + jj * 128, 128), hT[:, c, tl(t)], c == 0, c == 7, [rw[wbi], rh[c][t]], [rpb[gb_]])
                wu = wslot(); wg = wslot()
                for ((raw, rraw), bk, jx, wo) in ((uraw[p], ub, j, wu), (graw[p], gb_, 22 + j, wg)):
                    if t > 0:
                        S.op("pool", lambda e, raw=raw, jx=jx: e.tensor_copy(out=raw[:, 0:2], in_=fhalo[:, jx, :]), reads=[rfh], writes=rraw)
                    else:
                        S.op("pool", lambda e, raw=raw: e.memset(raw[:, 0:2], 0.0), writes=rraw)
                    S.op("act", lambda e, raw=raw, bk=bk: e.copy(out=raw[:, 2:514], in_=pb[bk][:, :]), reads=[rpb[bk]], writes=rraw)
                    S.op("act", lambda e, bk=bk, jx=jx, wo=wo: e.activation(out=work[:, wo, :], in_=pb[bk][:, :], func=AF.Identity,
                                                                            bias=fp(jx, 3), scale=fp(jx, 2)),
                         reads=[rpb[bk], rconst], writes=[rwk[wo]])
                    if t < NT - 1:
                        S.op("pool", lambda e, raw=raw, jx=jx: e.tensor_copy(out=fhalo[:, jx, :], in_=raw[:, 512:514]), reads=rraw, writes=[rfh])
                    for k in range(2):
                        S.op("dve", lambda e, raw=raw, jx=jx, wo=wo, k=k: e.scalar_tensor_tensor(
                            out=work[:, wo, :], in0=raw[:, k:k + TT], scalar=fp(jx, k), in1=work[:, wo, :], op0=ALU.mult, op1=ALU.add),
                            reads=rraw + [rconst, rwk[wo]], writes=[rwk[wo]])
                return (wu, wg, j)

            def back(st):
                wu, wg, j = st
                S.op("act", lambda e: e.activation(out=work[:, wg, :], in_=work[:, wg, :], func=AF.Gelu), reads=[rwk[wg]], writes=[rwk[wg]])
                S.op("dve", lambda e: e.tensor_tensor(out=actv(j), in0=work[:, wu, :], in1=work[:, wg, :], op=ALU.mult),
                     reads=[rwk[wu], rwk[wg]], writes=[ract(j)])

            for t in range(NT):
                pend = None
                for (j0, n) in groups:
                    wbi = nextw()
                    load_w(wbi, [("scr_up", l, j0 // 3)])
                    for jj in range(n):
                        st = front(t, wbi, jj, j0 + jj)
                        if pend is not None:
                            back(pend)
                        pend = st
                back(pend)
                for m in range(8):
                    wbi = nextw()
                    load_w(wbi, [("scr_dn", l, m)])
                    b = bank()
                    for j in range(22):
                        mm(pb[b][:, :], wsl(wbi, j, 128, 0, 128), actv(j), j == 0, j == 21, [rw[wbi], ract(j)], [rpb[b]])
                    S.op("dve", lambda e, b=b, m=m, t=t: e.tensor_tensor(out=xT[:, m, tl(t)], in0=pb[b][:, :], in1=xT[:, m, tl(t)], op=ALU.add),
                         reads=[rpb[b], rx[m][t]], writes=[rx[m][t]])

        for s in range(nseq):
            for c in range(8):
                S.dma("sp", xT[:, c, :], xT_d[s, c * 128:(c + 1) * 128, :], writes=rx[c])
            for l in range(nlayers):
                if s == 0:
                    convq.extend(conv_jobs(l))
                norm((l * 4 + G_MIX) * 8)
                if dbg and s == 0 and l == nlayers - 1:
                    dump("d_h", lambda c, t: hT[:, c, tl(t)], lambda c, t: rh[c][t])
                attn_mixer(l, "sb", OFF_SB, 0)
                attn_mixer(l, "fox", OFF_FOX, 1)
                attn_mixer(l, "dil", OFF_DIL, 2)
                lru(l)
                drip(100)
                if dbg and s == 0 and l == nlayers - 1:
                    dump("d_mixed", lambda c, t: mixedT[:, c, tl(t)], lambda c, t: rm[c][t])
                out_proj("w_out", 8, l)
                if dbg and s == 0 and l == nlayers - 1:
                    dump("d_x1", lambda c, t: xT[:, c, tl(t)], lambda c, t: rx[c][t])
                cross(l, s)
                if dbg and s == 0 and l == nlayers - 1:
                    dump("d_x2", lambda c, t: xT[:, c, tl(t)], lambda c, t: rx[c][t])
                ffn(l)
                if dbg and s == 0 and l == nlayers - 1:
                    dump("d_x3", lambda c, t: xT[:, c, tl(t)], lambda c, t: rx[c][t])
            norm(4 * L * 8, final_seq=s)
        S.finish()
        S.emit()
    return nc


def _t5_bucket(n):
    n = np.maximum(n, 0)
    max_exact = 16
    nf = np.maximum(n, 1).astype(np.float32)
    v = np.log(nf / np.float32(max_exact)) / np.float32(math.log(2048 / max_exact)) * np.float32(32 - max_exact)
    large = max_exact + v.astype(np.int32)
    large = np.minimum(large, 31)
    return np.where(n < max_exact, n, large)


def _host_consts():
    c = {}
    c["ident"] = np.eye(128, dtype=np.float32)
    j = np.arange(128)[:, None]; k = np.arange(128)[None, :]
    c["tri"] = (j >= k).astype(np.float32)
    kk = np.arange(128)[:, None]; jp = np.arange(897)[None, :]
    c["mask"] = (kk + 384 < jp).astype(np.float32)
    i = np.arange(TP); delta = i - 511
    valid = (delta >= 0) & (delta <= 2047)
    bk = _t5_bucket(delta)
    oh = np.zeros((32, TP), np.float32)
    oh[bk[valid], i[valid]] = 1.0
    c["onehot"] = oh
    mult = ((delta <= 128).astype(np.float32) + ((delta % 4 == 0) & (delta <= 512)).astype(np.float32)
            + ((delta % 16 == 0) & (delta <= 2048)).astype(np.float32)) * valid.astype(np.float32)
    c["mult"] = np.ascontiguousarray(np.broadcast_to(mult[None, :], (128, TP))).astype(np.float32)
    sel = np.zeros((4, 4 * 128), np.float32)
    for h in range(4):
        sel[h, h * 128:(h + 1) * 128] = 1.0
    c["sel"] = sel
    return c


def _pcol(vec):
    return np.ascontiguousarray(np.asarray(vec, np.float32).reshape(-1, 128).T)


def _host_params(inp):
    f = lambda a: np.ascontiguousarray(np.asarray(a, dtype=np.float32))
    p = {}
    for nm in ("w_in", "w_out", "w_cq", "w_ck", "w_cv", "w_co", "w_up", "w_down"):
        p[nm] = f(inp[nm])
    g = np.zeros((128, (4 * L + 1) * 8), np.float32)
    for l in range(L):
        for gi, nm in enumerate(("norm_mix_g", "norm_cross_g", "norm_mem_g", "norm_ffn_g")):
            g[:, (l * 4 + gi) * 8:(l * 4 + gi + 1) * 8] = _pcol(inp[nm][l])
    g[:, 4 * L * 8:] = _pcol(inp["final_norm_g"])
    p["gains"] = g
    p["fb"] = f(np.asarray(inp["b_forget"]).T)
    lr = np.zeros((128, L * 2 * 8), np.float32)
    for l in range(L):
        for c in range(2):
            sl = slice(c * 128, (c + 1) * 128)
            base = (l * 2 + c) * 8
            for k in range(4):
                lr[:, base + k] = np.asarray(inp["lru_conv_w"])[l, k, sl]
            lr[:, base + 4] = np.asarray(inp["lru_conv_b"])[l, sl]
            lr[:, base + 5] = np.asarray(inp["lru_b_a"])[l, sl]
            lr[:, base + 6] = np.asarray(inp["lru_b_x"])[l, sl]
            lr[:, base + 7] = np.asarray(inp["lru_lambda"])[l, sl]
    p["lrup"] = lr
    for nm, src in (("wabd", "lru_w_a"), ("wxbd", "lru_w_x")):
        w = np.zeros((L, 2, 128, 128), np.float32)
        a = np.asarray(inp[src], np.float32)
        for l in range(L):
            for c in range(2):
                for g2 in range(2):
                    w[l, c, g2 * 64:(g2 + 1) * 64, g2 * 64:(g2 + 1) * 64] = a[l, c * 2 + g2]
        p[nm] = w
    fp = np.zeros((128, L * 44 * 4), np.float32)
    cw = np.asarray(inp["ffn_conv_w"], np.float32); cb = np.asarray(inp["ffn_conv_b"], np.float32)
    for l in range(L):
        for jx in range(44):
            sl = slice(jx * 128, (jx + 1) * 128)
            base = (l * 44 + jx) * 4
            for k in range(3):
                fp[:, base + k] = cw[l, k, sl]
            fp[:, base + 3] = cb[l, sl]
    p["ffnp"] = fp
    p["relb"] = f(inp["rel_bias"])
    p.update(_host_consts())
    return p


_NC_CACHE = {}


def _get_nc(key, **kw):
    if key not in _NC_CACHE:
        rec = []
        build_nc(rec=rec, **kw)
        _NC_CACHE[key] = build_nc(plan=rec, **kw)
    return _NC_CACHE[key]


def run_cores(inp, ncores=8, dbg=False, nseq=2, nlayers=L):
    shared = _host_params(inp)
    x = np.asarray(inp["x"], np.float32); mem = np.asarray(inp["mem"], np.float32)
    in_maps = []
    for i in range(ncores):
        m = dict(shared)
        m["xT"] = np.ascontiguousarray(x[2 * i:2 * i + 2].transpose(0, 2, 1))
        m["memT"] = np.ascontiguousarray(mem[2 * i:2 * i + 2].transpose(0, 2, 1))
        in_maps.append(m)
    nc = _get_nc((dbg, nseq, nlayers), dbg=dbg, nseq=nseq, nlayers=nlayers)
    res = run_bass_kernel_spmd(nc, in_maps, core_ids=list(range(ncores)))
    return res


def kernel(**inputs):
    res = run_cores(inputs)
    out = np.empty((16, SEQ, DM), np.float32)
    for i, r in enumerate(res.results):
        out[2 * i:2 * i + 2] = np.asarray(r["yT"]).transpose(0, 2, 1)
    return out
```

```python
import math
import numpy as np
from contextlib import ExitStack
import concourse.bass as bass
import concourse.mybir as mybir
from concourse.bass_utils import run_bass_kernel_spmd

F32 = mybir.dt.float32
BF16 = mybir.dt.bfloat16
AF = mybir.ActivationFunctionType
ALU = mybir.AluOpType
AX = mybir.AxisListType

NDSEM = 48


class Res:
    __slots__ = ("name", "w", "rs")

    def __init__(self, name=""):
        self.name = name
        self.w = None
        self.rs = {}


class Sched:
    ENG = ("pe", "act", "dve", "pool", "sp")

    def __init__(self, nc, es):
        self.nc = nc
        self.sem = {e: es.enter_context(nc.semaphore("s_" + e)) for e in self.ENG}
        self.dsem = [es.enter_context(nc.semaphore("d%d" % i)) for i in range(NDSEM)]
        self.cnt = {e: 0 for e in self.ENG}
        self.ops = {e: [] for e in self.ENG}
        self.seen = {e: {} for e in self.ENG}
        self.dcnt = [0] * NDSEM
        self.dnext = 0

    def _collect(self, eng, reads, writes):
        waits = {}
        seen = self.seen[eng]

        def need(dep, same_ok):
            if dep is None:
                return
            k, v = dep
            if k == eng and same_ok:
                return
            if seen.get(k, 0) >= v:
                return
            if waits.get(k, 0) < v:
                waits[k] = v
        for r in reads:
            need(r.w, eng == "pe")
        for w in writes:
            need(w.w, eng == "pe")
            for d in w.rs.items():
                need(d, eng == "pe")
        for k, v in waits.items():
            seen[k] = v
        return list(waits.items())

    def op(self, eng, fn, reads=(), writes=()):
        waits = self._collect(eng, reads, writes)
        self.cnt[eng] += 1
        me = (eng, self.cnt[eng])
        self.ops[eng].append((fn, waits, None))
        for r in reads:
            if r.rs.get(eng, 0) < me[1]:
                r.rs[eng] = me[1]
        for w in writes:
            w.w = me
            w.rs = {}

    def dma(self, q, out, in_, reads=(), writes=(), **kw):
        idx = self.dnext
        self.dnext = (self.dnext + 1) % NDSEM
        prev = self.dcnt[idx]
        self.dcnt[idx] += 16
        val = self.dcnt[idx]
        waits = self._collect(q, reads, writes)
        key = "d%d" % idx
        if prev > 0 and self.seen[q].get(key, 0) < prev:
            waits.append((key, prev))
            self.seen[q][key] = prev
        self.ops[q].append((lambda e: e.dma_start(out=out, in_=in_, **kw), waits, idx))
        for r in reads:
            r.rs[key] = val
        for w in writes:
            w.w = (key, val)
            w.rs = {}

    def handle(self, k):
        if k in self.sem:
            return self.sem[k]
        return self.dsem[int(k[1:])]

    def finish(self):
        waits = []
        for i in range(NDSEM):
            if self.dcnt[i] > 0:
                waits.append(("d%d" % i, self.dcnt[i]))
        for e in self.ENG:
            if e != "sp" and self.cnt[e] > 0:
                waits.append((e, self.cnt[e]))
        self.ops["sp"].append((None, waits, None))

    def barrier(self):
        for e in self.ENG:
            waits = []
            for i in range(NDSEM):
                k = "d%d" % i
                if self.dcnt[i] > 0 and self.seen[e].get(k, 0) < self.dcnt[i]:
                    waits.append((k, self.dcnt[i]))
                    self.seen[e][k] = self.dcnt[i]
            for o in self.ENG:
                if o != e and self.cnt[o] > 0 and self.seen[e].get(o, 0) < self.cnt[o]:
                    waits.append((o, self.cnt[o]))
                    self.seen[e][o] = self.cnt[o]
            if waits:
                self.ops[e].append((None, waits, None))

    def emit(self):
        nc = self.nc
        with nc.Block() as block:
            def mk(e):
                def body(engobj):
                    for fn, waits, dm in self.ops[e]:
                        for k, v in waits:
                            engobj.wait_ge(self.handle(k), v)
                        if fn is None:
                            continue
                        ins = fn(engobj)
                        if dm is None:
                            ins.then_inc(self.sem[e], 1)
                        else:
                            ins.then_inc(self.dsem[dm], 16)
                return body
            block.tensor(mk("pe"))
            block.scalar(mk("act"))
            block.vector(mk("dve"))
            block.gpsimd(mk("pool"))
            block.sync(mk("sp"))

L = 2; SEQ = 2048; DM = 1024; NT = 4; TT = 512; NIN = 2820; DFF = 2816; NMEM = 256
OFF_SB, OFF_FOX, OFF_F, OFF_DIL, OFF_LX, OFF_LG = 0, 768, 1536, 1540, 2308, 2564
WCOLS = 6144
EW = 2432
TP = 2560
NWORK = 6
G_MIX, G_CROSS, G_MEM, G_FFN = 0, 1, 2, 3


def build_nc(nseq=2, nlayers=L, dbg=False, plan=None, rec=None):
    nc = bass.Bass("TRN2", target_bir_lowering=False)

    def din(name, shape, dt=F32):
        return nc.dram_tensor(name, shape, dt, kind="ExternalInput").ap()
    xT_d = din("xT", [2, DM, SEQ]); memT_d = din("memT", [2, DM, NMEM])
    w_in_d = din("w_in", [L, DM, NIN]); w_out_d = din("w_out", [L, DM, DM])
    w_cq_d = din("w_cq", [L, DM, 256]); w_ck_d = din("w_ck", [L, DM, 256]); w_cv_d = din("w_cv", [L, DM, 256])
    w_co_d = din("w_co", [L, 256, DM]); w_up_d = din("w_up", [L, DM, 2 * DFF]); w_down_d = din("w_down", [L, DFF, DM])
    gains_d = din("gains", [128, (4 * L + 1) * 8]); fb_d = din("fb", [4, L])
    lrup_d = din("lrup", [128, L * 2 * 8]); wabd_d = din("wabd", [L, 2, 128, 128]); wxbd_d = din("wxbd", [L, 2, 128, 128])
    ffnp_d = din("ffnp", [128, L * 44 * 4]); relb_d = din("relb", [32, 4])
    ident_d = din("ident", [128, 128]); tri_d = din("tri", [128, 128]); mask_d = din("mask", [128, 897])
    onehot_d = din("onehot", [32, TP]); mult_d = din("mult", [128, TP]); sel_d = din("sel", [4, 4 * 128])
    yT_d = nc.dram_tensor("yT", [2, DM, SEQ], F32, kind="ExternalOutput").ap()
    tpad_d = nc.dram_tensor("tpad", [4, 128, TP], BF16, kind="Internal").ap()
    wup_s = nc.dram_tensor("wup_s", [L, 8, 128, WCOLS], BF16, kind="Internal").ap()
    wdn_s = nc.dram_tensor("wdn_s", [L, 8, 128, 22 * 128], BF16, kind="Internal").ap()
    dbg_d = {}
    if dbg:
        for nm in ("d_h", "d_mixed", "d_x1", "d_x2", "d_x3"):
            dbg_d[nm] = nc.dram_tensor(nm, [DM, SEQ], F32, kind="ExternalOutput").ap()

    es = ExitStack()
    with es:
        S = Sched(nc, es)

        def sb(name, shape, dt):
            return es.enter_context(nc.sbuf_tensor(name, shape, dt))
        xT = sb("xTt", [128, 8, SEQ], F32); rx = [[Res() for _ in range(NT)] for _ in range(8)]
        hT = sb("hTt", [128, 8, SEQ], BF16); rh = [[Res() for _ in range(NT)] for _ in range(8)]
        mixedT = sb("mixT", [128, 8, SEQ], BF16); rm = [[Res() for _ in range(NT)] for _ in range(8)]
        qT = sb("qTt", [128, SEQ], BF16); rq = [Res() for _ in range(NT)]
        kT = sb("kTt", [128, SEQ], BF16); rk = [Res() for _ in range(NT)]
        vaug = sb("vaug", [128, 16, 2, 128], BF16); rv = [Res() for _ in range(4)]
        xr = qT[:, 0:1032].bitcast(F32); gr = kT[:, 0:1032].bitcast(F32); rxr = rq; rgr = rk
        wbuf = [sb("wbuf%d" % i, [128, WCOLS], BF16) for i in range(2)]; rw = [Res(), Res()]
        work = sb("work", [128, NWORK, TT], F32); rwk = [Res() for _ in range(NWORK)]
        pbf = sb("pbf", [128, 6, TT], BF16); rpbf = [Res() for _ in range(6)]
        estrip = sb("estrip", [128, EW], BF16); res_ = Res()
        memn = sb("memn", [128, 8, NMEM], BF16); rmemn = Res()
        rrow = sb("rrow", [1, TT], F32); rrr = Res()
        rbf = sb("rbf", [128, TT], BF16); rrb = Res()
        ones_r0 = sb("ones_r0", [128, 128], BF16)
        selb = sb("selb", [128, 4 * 128], BF16)
        ones_bf = sb("ones_bf", [128, 128], BF16); ones_f = sb("ones_f", [128, 128], F32)
        tri = sb("trib", [128, 128], BF16); ident = sb("identf", [128, 128], F32)
        maskb = sb("maskb", [128, 897], BF16)
        gains = sb("gainst", [128, (4 * L + 1) * 8], F32); fbn = sb("fbn", [4, L], F32)
        lrup = sb("lrupt", [128, L * 2 * 8], F32); coef = sb("coef", [128, 2], F32); hlast = sb("hlast", [128, 2], F32)
        wabd = sb("wabdt", [128, 2, 128], F32); wxbd = sb("wxbdt", [128, 2, 128], F32); rwab = Res()
        ffnp = sb("ffnpt", [128, L * 44 * 4], F32); fhalo = sb("fhalo", [128, 44, 2], F32); rfh = Res()
        ckneg = sb("ckneg", [128, 64], F32); rck = Res()
        cst = sb("cst", [128, 2], F32)
        stage = sb("stage", [32, 136], F32); rstage = Res()
        rconst = Res(); rcoef = Res(); rhl = [Res(), Res()]; rtpad = Res()
        pb = [es.enter_context(nc.psum_tensor("pb%d" % i, [128, TT], F32)) for i in range(8)]
        rpb = [Res() for _ in range(8)]

        ctr = {"b": 0, "w": 0, "p": 0, "wb": 0, "ld": 0}
        rs_up = [[Res() for _ in range(8)] for _ in range(L)]
        rs_dn = [[Res() for _ in range(8)] for _ in range(L)]
        FGROUPS = [(g * 3, min(3, 22 - g * 3)) for g in range(8)]
        convq = []

        def conv_jobs(l):
            jobs = []
            for g, (j0, n) in enumerate(FGROUPS):
                def job(g=g, j0=j0, n=n):
                    dst = wup_s[l, g].rearrange("p (c n) -> p c n", c=8)
                    S.dma("pool", dst[:, :, 0:n * 128], w_up_d[l][:, j0 * 128:(j0 + n) * 128].rearrange("(c p) n -> p c n", p=128),
                          writes=[rs_up[l][g]])
                    S.dma("pool", dst[:, :, 384:384 + n * 128],
                          w_up_d[l][:, DFF + j0 * 128:DFF + (j0 + n) * 128].rearrange("(c p) n -> p c n", p=128), writes=[rs_up[l][g]])
                jobs.append(job)
            for m in range(8):
                def job(m=m):
                    S.dma("pool", wdn_s[l, m].rearrange("p (c n) -> p c n", c=22),
                          w_down_d[l][:, m * 128:(m + 1) * 128].rearrange("(c p) n -> p c n", p=128), writes=[rs_dn[l][m]])
                jobs.append(job)
            return jobs

        def drip(k):
            for _ in range(k):
                if convq:
                    convq.pop(0)()
        dram_w = {"w_in": w_in_d, "w_out": w_out_d, "w_cq": w_cq_d, "w_ck": w_ck_d, "w_cv": w_cv_d, "w_co": w_co_d, "w_up": w_up_d, "w_down": w_down_d}

        def bank():
            ctr["b"] = (ctr["b"] + 1) % 8
            return ctr["b"]

        wmin = {"v": 0}

        def wslot():
            ctr["w"] += 1
            if ctr["w"] >= NWORK or ctr["w"] < wmin["v"]:
                ctr["w"] = wmin["v"]
            return ctr["w"]

        def pslot():
            ctr["p"] = (ctr["p"] + 1) % 2
            return ctr["p"]

        def nextw():
            ctr["wb"] = (ctr["wb"] + 1) % 2
            return ctr["wb"]

        def mm(out, lhsT, rhs, start, stop, reads, writes):
            S.op("pe", lambda e: e.matmul(out, lhsT=lhsT, rhs=rhs, start=start, stop=stop), reads=reads, writes=writes)

        def tl(t):
            return slice(t * TT, (t + 1) * TT)

        S.dma("sp", ident[:, :], ident_d, writes=[rconst])
        S.dma("pool", tri[:, :], tri_d, writes=[rconst])
        S.dma("pool", maskb[:, :], mask_d, writes=[rconst])
        S.op("dve", lambda e: e.memset(selb[:, :], 0.0), writes=[rconst])
        S.op("dve", lambda e: e.memset(estrip[:, :], 0.0), writes=[res_])
        S.dma("pool", selb[0:4, :], sel_d, writes=[rconst])
        S.dma("sp", gains[:, :], gains_d, writes=[rconst])
        S.dma("sp", fbn[:, :], fb_d, writes=[rconst])
        S.dma("sp", lrup[:, :], lrup_d, writes=[rconst])
        S.dma("sp", ffnp[:, :], ffnp_d, writes=[rconst])
        S.op("dve", lambda e: e.memset(ones_bf[:, :], 1.0), writes=[rconst])
        S.op("dve", lambda e: e.memset(ones_f[:, :], 1.0), writes=[rconst])
        S.op("dve", lambda e: e.memset(rbf[:, :], 0.0), writes=[rrb])
        S.op("dve", lambda e: e.memset(ones_r0[:, :], 0.0), writes=[rconst])
        S.op("dve", lambda e: e.memset(ones_r0[0:1, :], 1.0), writes=[rconst])
        S.op("dve", lambda e: e.memset(cst[:, 0:1], 1e-6), writes=[rconst])
        S.op("dve", lambda e: e.memset(cst[:, 1:2], 1.0), writes=[rconst])
        S.op("dve", lambda e: e.memset(vaug[:, :, :, :], 0.0), writes=rv)
        S.op("dve", lambda e: e.memset(vaug[:, :, 0, 64:65], 1.0), writes=rv)
        S.op("dve", lambda e: e.memset(vaug[:, :, 1, 0:1], 1.0), writes=rv)
        S.op("dve", lambda e: e.tensor_scalar(out=fbn[:, :], in0=fbn[:, :], scalar1=-1.0, scalar2=None, op0=ALU.mult),
             reads=[rconst], writes=[rconst])
        relb = stage[0:32, 0:4]; relrep = stage[0:32, 8:136]
        xTf = xT[:, :, :].rearrange("p c s -> p (c s)"); hTf = hT[:, :, :].rearrange("p c s -> p (c s)")
        multt = xTf[:, 0:TP]; tf32 = xTf[:, 4096:4096 + TP]; tb16 = hTf[:, 0:TP]
        S.dma("sp", relb, relb_d, writes=[rstage])
        S.dma("sp", work[0:32, 0:5, :], onehot_d.rearrange("b (j n) -> b j n", j=5), writes=rwk)
        S.dma("sp", multt, mult_d, writes=rx[0] + rx[1])
        for h in range(4):
            S.op("dve", lambda e, h=h: e.tensor_copy(out=relrep, in_=stage[0:32, h:h + 1].to_broadcast([32, 128])),
                 reads=[rstage], writes=[rstage])
            for j in range(5):
                b = bank()
                mm(pb[b][:, :], relrep, work[0:32, j, :], True, True, [rstage] + rwk, [rpb[b]])
                S.op("act", lambda e, b=b, j=j: e.activation(out=tf32[:, j * TT:(j + 1) * TT], in_=pb[b][:, :], func=AF.Exp),
                     reads=[rpb[b]], writes=rx[2] + rx[3])
            S.op("dve", lambda e: e.tensor_tensor(out=tb16, in0=tf32, in1=multt, op=ALU.mult),
                 reads=rx[0] + rx[1] + rx[2] + rx[3], writes=rh[0] + rh[1])
            S.dma("sp", tpad_d[h], tb16, reads=rh[0] + rh[1], writes=[rtpad])

        def issue_load(wbi, pieces):
            if pieces[0][0] == "scr_up":
                _, l_, g = pieces[0]
                S.dma("sp", wbuf[wbi][:, 0:WCOLS], wup_s[l_, g], reads=[rs_up[l_][g]], writes=[rw[wbi]])
                return
            if pieces[0][0] == "scr_dn":
                _, l_, m = pieces[0]
                S.dma("sp", wbuf[wbi][:, 0:22 * 128], wdn_s[l_, m], reads=[rs_dn[l_][m]], writes=[rw[wbi]])
                return
            for ((nm, l_, a, b_), c0, nch, ncols, cstride) in pieces:
                src = dram_w[nm][l_][:, a:b_]
                full = wbuf[wbi][:, 0:nch * cstride].rearrange("p (c n) -> p c n", c=nch)[:, :, c0:c0 + ncols]
                S.dma("pool", full, src.rearrange("(c p) n -> p c n", p=128), writes=[rw[wbi]])

        def load_w(wbi, pieces):
            i = ctr["ld"]
            ctr["ld"] += 1
            if plan is None:
                rec.append((wbi, pieces))
                issue_load(wbi, pieces)
                return
            if i == 0:
                issue_load(*plan[0])
            if i + 1 < len(plan):
                issue_load(*plan[i + 1])

        def wsl(wbi, c, cstride, c0, n):
            return wbuf[wbi][:, c * cstride + c0: c * cstride + c0 + n]

        def norm(goff, final_seq=None):
            for c in range(8):
                S.op("act", lambda e, c=c: e.activation(out=hT[:, c, :], in_=xT[:, c, :], func=AF.Square),
                     reads=rx[c], writes=rh[c])
            for t in range(NT):
                b = bank()
                for c in range(8):
                    mm(pb[b][:, :], ones_bf[:, :], hT[:, c, tl(t)], c == 0, c == 7, [rconst, rh[c][t]], [rpb[b]])
                ws = wslot()
                S.op("act", lambda e, b=b, ws=ws: e.activation(out=work[:, ws, :], in_=pb[b][:, :], func=AF.Ln,
                                                               bias=cst[:, 0:1], scale=1.0 / DM),
                     reads=[rpb[b], rconst], writes=[rwk[ws]])
                S.op("act", lambda e, ws=ws: e.activation(out=work[:, ws, :], in_=work[:, ws, :], func=AF.Exp, scale=-0.5),
                     reads=[rwk[ws]], writes=[rwk[ws]])
                for c in range(8):
                    if final_seq is None:
                        S.op("dve", lambda e, c=c, t=t, ws=ws: e.scalar_tensor_tensor(
                            out=hT[:, c, tl(t)], in0=xT[:, c, tl(t)], scalar=gains[:, goff + c:goff + c + 1],
                            in1=work[:, ws, :], op0=ALU.mult, op1=ALU.mult),
                            reads=[rx[c][t], rwk[ws], rconst], writes=[rh[c][t]])
                    else:
                        w2 = wslot()
                        while w2 == ws:
                            w2 = wslot()
                        S.op("dve", lambda e, c=c, t=t, ws=ws, w2=w2: e.scalar_tensor_tensor(
                            out=work[:, w2, :], in0=xT[:, c, tl(t)], scalar=gains[:, goff + c:goff + c + 1],
                            in1=work[:, ws, :], op0=ALU.mult, op1=ALU.mult),
                            reads=[rx[c][t], rwk[ws], rconst], writes=[rwk[w2]])
                        S.dma("sp", yT_d[final_seq, c * 128:(c + 1) * 128, tl(t)], work[:, w2, :], reads=[rwk[w2]])

        def dump(name, src_fn, reads_fn):
            if not dbg:
                return
            for c in range(8):
                for t in range(NT):
                    w2 = wslot()
                    S.op("dve", lambda e, c=c, t=t, w2=w2: e.tensor_copy(out=work[:, w2, :], in_=src_fn(c, t)),
                         reads=[reads_fn(c, t)], writes=[rwk[w2]])
                    S.dma("sp", dbg_d[name][c * 128:(c + 1) * 128, tl(t)], work[:, w2, :], reads=[rwk[w2]])

        def proj_fm(wbi, cstride, c0, M, evac):
            for t in range(NT):
                b = bank()
                for c in range(8):
                    mm(pb[b][0:M, :], wsl(wbi, c, cstride, c0, M), hT[:, c, tl(t)], c == 0, c == 7, [rw[wbi], rh[c][t]], [rpb[b]])
                evac(t, b)

        def proj_v(wbi, cstride, c0):
            for g in range(4):
                b = bank()
                for i in range(4):
                    blk = g * 4 + i
                    for c in range(8):
                        mm(pb[b][:, i * 128:(i + 1) * 128], hT[:, c, blk * 128:(blk + 1) * 128], wsl(wbi, c, cstride, c0, 128),
                           c == 0, c == 7, [rw[wbi], rh[c][g]], [rpb[b]])
                pv = pb[b][:, :].rearrange("p (b n) -> p b n", b=4)
                S.op("dve", lambda e, g=g, pv=pv: e.tensor_copy(out=vaug[:, g * 4:(g + 1) * 4, 0, 0:64], in_=pv[:, :, 0:64]),
                     reads=[rpb[b]], writes=[rv[g]])
                S.op("act", lambda e, g=g, pv=pv: e.copy(out=vaug[:, g * 4:(g + 1) * 4, 1, 64:128], in_=pv[:, :, 64:128]),
                     reads=[rpb[b]], writes=[rv[g]])

        def evac_q(t, b):
            S.op("act", lambda e: e.activation(out=qT[:, tl(t)], in_=pb[b][:, :], func=AF.Copy, scale=0.125),
                 reads=[rpb[b]], writes=[rq[t]])

        def evac_k(t, b):
            S.op("dve", lambda e: e.tensor_copy(out=kT[:, tl(t)], in_=pb[b][:, :]), reads=[rpb[b]], writes=[rk[t]])

        def finalize(hh, ci, t, ob, normalize, mb=None):
            rows = slice(0, 64) if hh == 0 else slice(64, 128)
            if not normalize:
                S.op("act", lambda e: e.copy(out=mixedT[rows, ci, tl(t)], in_=pb[ob][rows, :]), reads=[rpb[ob]], writes=[rm[ci][t]])
                return
            w1 = fin1(hh, ob)
            fin2(hh, ci, t, ob, w1, MB[0] if mb is None else mb)

        def fin1(hh, ob):
            drow = 64 if hh == 0 else 0
            w1 = wslot()
            S.op("act", lambda e: e.activation(out=work[drow:drow + 1, w1, :], in_=pb[ob][drow:drow + 1, :], func=AF.Ln),
                 reads=[rpb[ob]], writes=[rwk[w1]])
            S.op("act", lambda e: e.activation(out=work[drow:drow + 1, w1, :], in_=work[drow:drow + 1, w1, :], func=AF.Exp, scale=-1.0),
                 reads=[rwk[w1]], writes=[rwk[w1]])
            return w1

        def fin2(hh, ci, t, ob, w1, mb):
            rows = slice(0, 64) if hh == 0 else slice(64, 128)
            drow = 64 if hh == 0 else 0
            ncol = 64 if hh == 0 else 128
            mm(pb[mb][0:ncol, :], ones_f[drow:drow + 1, 0:ncol], work[drow:drow + 1, w1, :], True, True, [rconst, rwk[w1]], [rpb[mb]])
            w2 = wslot()
            S.op("act", lambda e: e.copy(out=work[rows, w2, :], in_=pb[mb][rows, :]), reads=[rpb[mb]], writes=[rwk[w2]])
            S.op("dve", lambda e: e.tensor_tensor(out=mixedT[rows, ci, tl(t)], in0=pb[ob][rows, :], in1=work[rows, w2, :], op=ALU.mult),
                 reads=[rpb[ob], rwk[w2]], writes=[rm[ci][t]])

        OB = [0, 1]; ZB = [2, 3]; CB = [4, 5, 6]; MB = [7]; ZB4 = [2, 3, 4, 5, 6]
        rot = {"ob": 0, "zb": 0, "cb": 0, "pl": 0, "pa": 0, "pp": 0, "pg": 0, "zb4": 0}

        def take(key, pool):
            rot[key] = (rot[key] + 1) % len(pool)
            return pool[rot[key]]
        DEPTH = 2

        def attn(kind, hh, ci, head=0, ctr_res=None):
            base = hh * 64
            for t in range(NT):
                ob = take("ob", OB)
                if kind == "cross":
                    kbs = [0, 1]
                elif kind == "sb":
                    kbs = list(range(4 * (t + 1) - 1, -1, -1))
                else:
                    kbs = list(range(4 * (t + 1)))
                nk = len(kbs)

                def stZ(idx, kb):
                    first = idx == 0
                    last = idx == nk - 1
                    diag = (kind != "cross") and kb >= 4 * t
                    o = kb * 128 - t * TT
                    zb = take("zb4", ZB4)
                    if kind == "cross":
                        kcol = slice(ci * 256 + kb * 128, ci * 256 + (kb + 1) * 128)
                        vl = vaug[:, kb * 2 + ci, hh, :]
                        rkk, rvv = rk[0], rv[0]
                    else:
                        kcol = slice(kb * 128, (kb + 1) * 128)
                        vl = vaug[:, kb, hh, :]
                        rkk, rvv = rk[kb // 4], rv[kb // 4]
                    st = {"first": first, "last": last, "diag": diag, "o": o, "vl": vl, "rvv": rvv, "zb": zb, "kb": kb}
                    mm(pb[zb][:, :], kT[base:base + 64, kcol], qT[base:base + 64, tl(t)], True, kind != "fox",
                       [rkk, rq[t]], [rpb[zb]])
                    if kind == "fox":
                        mm(pb[zb][:, :], selb[:, head * 128:(head + 1) * 128], estrip[:, tl(t)], False, True,
                           [rconst, res_], [rpb[zb]])
                    return st

                def stA(st, sbmid=None):
                    first, last, diag, o, zb, kb = st["first"], st["last"], st["diag"], st["o"], st["zb"], st["kb"]
                    if kind == "sb":
                        wsp = wslot()
                        S.op("act", lambda e: e.activation(out=work[:, wsp, :], in_=pb[zb][:, :], func=AF.Exp),
                             reads=[rpb[zb]], writes=[rwk[wsp]])
                        if sbmid is not None:
                            sbmid()
                        S.op("act", lambda e: e.activation(out=work[:, wsp, :], in_=work[:, wsp, :], func=AF.Ln,
                                                           bias=cst[:, 1:2], scale=1.0),
                             reads=[rwk[wsp], rconst], writes=[rwk[wsp]])
                        ls = take("pl", [0, 1]); Lb = pbf[:, ls, :]
                        st["ls"] = ls
                        if diag:
                            S.op("dve", lambda e: e.scalar_tensor_tensor(
                                out=Lb, in0=work[:, wsp, :], scalar=-1.0, in1=maskb[:, 384 - o:384 - o + TT], op0=ALU.mult, op1=ALU.mult),
                                reads=[rwk[wsp], rconst], writes=[rpbf[ls]])
                        else:
                            S.op("dve", lambda e: e.tensor_scalar(out=Lb, in0=work[:, wsp, :], scalar1=-1.0, scalar2=None,
                                                                  op0=ALU.mult), reads=[rwk[wsp]], writes=[rpbf[ls]])
                    else:
                        ps_ = take("pp", [0, 1, 2])
                        P = pbf[:, ps_, :]
                        st["ps"] = ps_
                        if kind == "fox":
                            if diag:
                                wa = wslot()
                                S.op("dve", lambda e: e.tensor_scalar(
                                    out=work[:, wa, :], in0=pb[zb][:, :], scalar1=ckneg[:, kb * 4 + head:kb * 4 + head + 1], scalar2=80.0,
                                    op0=ALU.add, op1=ALU.min), reads=[rpb[zb], rck], writes=[rwk[wa]])
                                S.op("act", lambda e: e.activation(out=P, in_=work[:, wa, :], func=AF.Exp),
                                     reads=[rwk[wa]], writes=[rpbf[ps_]])
                                S.op("dve", lambda e: e.tensor_tensor(out=P, in0=P, in1=maskb[:, 385 - o:385 - o + TT], op=ALU.mult),
                                     reads=[rpbf[ps_], rconst], writes=[rpbf[ps_]])
                            else:
                                S.op("act", lambda e: e.activation(out=P, in_=pb[zb][:, :], func=AF.Exp,
                                                                   bias=ckneg[:, kb * 4 + head:kb * 4 + head + 1], scale=1.0),
                                     reads=[rpb[zb], rck], writes=[rpbf[ps_]])
                        else:
                            S.op("act", lambda e: e.activation(out=P, in_=pb[zb][:, :], func=AF.Exp),
                                 reads=[rpb[zb]], writes=[rpbf[ps_]])
                            if kind == "dil":
                                d = -o
                                S.op("dve", lambda e: e.tensor_tensor(out=P, in0=P, in1=estrip[:, d + 384:d + 384 + TT], op=ALU.mult),
                                     reads=[rpbf[ps_], res_], writes=[rpbf[ps_]])
                    return st

                def stB(st):
                    if kind != "sb":
                        return
                    first, last, ls = st["first"], st["last"], st["ls"]
                    Lb = pbf[:, ls, :]
                    cb = st["zb"]
                    st["cb"] = cb
                    rb_ = MB[0]
                    if not last:
                        mm(pb[rb_][:, :], ones_bf[:, :], Lb, first, True, [rconst, rpbf[ls]], [rpb[rb_]])
                    mm(pb[cb][:, :], tri[:, :], Lb, False, first, [rconst, rpbf[ls]], [rpb[cb]])
                    if not first:
                        mm(pb[cb][:, :], ones_r0[:, :], rbf[:, :], False, True, [rconst, rrb], [rpb[cb]])
                    if not last:
                        S.op("dve", lambda e: e.tensor_copy(out=rbf[0:1, :], in_=pb[rb_][0:1, :]), reads=[rpb[rb_]], writes=[rrb])

                def stC1(st):
                    diag, o = st["diag"], st["o"]
                    if kind == "sb":
                        cb = st["cb"]
                        ps_ = take("pa", [4, 5])
                        st["ps"] = ps_
                        P = pbf[:, ps_, :]
                        S.op("act", lambda e: e.activation(out=P, in_=pb[cb][:, :], func=AF.Exp),
                             reads=[rpb[cb]], writes=[rpbf[ps_]])
                        if diag:
                            S.op("dve", lambda e: e.tensor_tensor(out=P, in0=P, in1=maskb[:, 384 - o:384 - o + TT], op=ALU.mult),
                                 reads=[rpbf[ps_], rconst], writes=[rpbf[ps_]])

                def stC2(st):
                    first, last, vl, rvv = st["first"], st["last"], st["vl"], st["rvv"]
                    ps_ = st["ps"]
                    mm(pb[ob][:, :], vl, pbf[:, ps_, :], first, last, [rvv, rpbf[ps_]], [rpb[ob]])

                sts = []
                for n in range(-1, nk + 2):
                    if 0 <= n + 1 < nk:
                        sts.append(stZ(n + 1, kbs[n + 1]))
                    c1 = (lambda n=n: stC1(sts[n - 2])) if 0 <= n - 2 < nk else None
                    if 0 <= n < nk:
                        stA(sts[n], c1 if kind == "sb" else None)
                        if kind != "sb" and c1 is not None:
                            c1()
                    elif c1 is not None:
                        c1()
                    if 0 <= n - 1 < nk:
                        stB(sts[n - 1])
                    if 0 <= n - 2 < nk:
                        stC2(sts[n - 2])
                finalize(hh, ci, t, ob, kind != "sb")

        def attn_sb(ci):
            items = []
            seg = 0
            for hh in range(2):
                for t in range(NT):
                    kbs = list(range(4 * (t + 1) - 1, -1, -1))
                    for idx, kb in enumerate(kbs):
                        items.append({"hh": hh, "t": t, "idx": idx, "kb": kb, "first": idx == 0, "last": idx == len(kbs) - 1,
                                      "diag": kb >= 4 * t, "o": kb * 128 - t * TT, "ob": OB[seg % 2]})
                    seg += 1
            n_it = len(items)

            def stZ(st):
                hh, t, kb = st["hh"], st["t"], st["kb"]
                base = hh * 64
                zb = take("zb4", ZB4)
                st["zb"] = zb
                mm(pb[zb][:, :], kT[base:base + 64, kb * 128:(kb + 1) * 128], qT[base:base + 64, tl(t)], True, True,
                   [rk[kb // 4], rq[t]], [rpb[zb]])

            def stA(st, mid):
                zb, diag, o = st["zb"], st["diag"], st["o"]
                wsp = wslot()
                S.op("act", lambda e: e.activation(out=work[:, wsp, :], in_=pb[zb][:, :], func=AF.Exp),
                     reads=[rpb[zb]], writes=[rwk[wsp]])
                if mid is not None:
                    mid()
                S.op("act", lambda e: e.activation(out=work[:, wsp, :], in_=work[:, wsp, :], func=AF.Ln, bias=cst[:, 1:2], scale=1.0),
                     reads=[rwk[wsp], rconst], writes=[rwk[wsp]])
                ls = take("pl", [0, 1]); Lb = pbf[:, ls, :]
                st["ls"] = ls
                if diag:
                    S.op("dve", lambda e: e.scalar_tensor_tensor(
                        out=Lb, in0=work[:, wsp, :], scalar=-1.0, in1=maskb[:, 384 - o:384 - o + TT], op0=ALU.mult, op1=ALU.mult),
                        reads=[rwk[wsp], rconst], writes=[rpbf[ls]])
                else:
                    S.op("dve", lambda e: e.tensor_scalar(out=Lb, in0=work[:, wsp, :], scalar1=-1.0, scalar2=None, op0=ALU.mult),
                         reads=[rwk[wsp]], writes=[rpbf[ls]])

            def stB(st):
                first, last, ls, cb = st["first"], st["last"], st["ls"], st["zb"]
                Lb = pbf[:, ls, :]
                rb_ = MB[0]
                if not last:
                    mm(pb[rb_][:, :], ones_bf[:, :], Lb, first, True, [rconst, rpbf[ls]], [rpb[rb_]])
                mm(pb[cb][:, :], tri[:, :], Lb, False, first, [rconst, rpbf[ls]], [rpb[cb]])
                if not first:
                    mm(pb[cb][:, :], ones_r0[:, :], rbf[:, :], False, True, [rconst, rrb], [rpb[cb]])
                if not last:
                    S.op("dve", lambda e: e.tensor_copy(out=rbf[0:1, :], in_=pb[rb_][0:1, :]), reads=[rpb[rb_]], writes=[rrb])

            def stC1(st):
                diag, o, cb = st["diag"], st["o"], st["zb"]
                ps_ = take("pa", [4, 5])
                st["ps"] = ps_
                P = pbf[:, ps_, :]
                S.op("act", lambda e: e.activation(out=P, in_=pb[cb][:, :], func=AF.Exp), reads=[rpb[cb]], writes=[rpbf[ps_]])
                if diag:
                    S.op("dve", lambda e: e.tensor_tensor(out=P, in0=P, in1=maskb[:, 384 - o:384 - o + TT], op=ALU.mult),
                         reads=[rpbf[ps_], rconst], writes=[rpbf[ps_]])

            def stC2(st):
                hh, t, kb, ob, ps_ = st["hh"], st["t"], st["kb"], st["ob"], st["ps"]
                mm(pb[ob][:, :], vaug[:, kb, hh, :], pbf[:, ps_, :], st["first"], st["last"], [rv[kb // 4], rpbf[ps_]], [rpb[ob]])
                if st["last"]:
                    finalize(hh, ci, t, ob, False)

            for n in range(-1, n_it + 2):
                if 0 <= n + 1 < n_it:
                    stZ(items[n + 1])
                c1 = (lambda n=n: stC1(items[n - 2])) if 0 <= n - 2 < n_it else None
                if 0 <= n < n_it:
                    stA(items[n], c1)
                elif c1 is not None:
                    c1()
                if 0 <= n - 1 < n_it:
                    stB(items[n - 1])
                if 0 <= n - 2 < n_it:
                    stC2(items[n - 2])

        estripB = work[:, 0:3, :].rearrange("p a s -> p (a s)").bitcast(BF16)[:, 0:EW]

        def attn2(kind, ci, head0, ctr_res=None):
            HD = []
            for hh in range(2):
                HD.append({"hh": hh, "base": hh * 64, "head": head0 + hh, "ob": hh, "zp": [2, 3, 4] if hh == 0 else [5, 6, 7],
                           "pp": [0, 1, 2] if hh == 0 else [3, 4, 5], "zi": 0, "pi": 0,
                           "strip": (estrip, [res_]) if hh == 0 else (estripB, rwk[0:3])})
            pend = {"f": []}
            for t in range(NT):
                kbs = [0, 1] if kind == "cross" else list(range(4 * (t + 1)))
                nk = len(kbs)

                def stZ1(H, idx, kb):
                    hh, base = H["hh"], H["base"]
                    H["zi"] = (H["zi"] + 1) % 3
                    zb = H["zp"][H["zi"]]
                    if kind == "cross":
                        kcol = slice(ci * 256 + kb * 128, ci * 256 + (kb + 1) * 128)
                        vl = vaug[:, kb * 2 + ci, hh, :]
                        rkk, rvv = rk[0], rv[0]
                    else:
                        kcol = slice(kb * 128, (kb + 1) * 128)
                        vl = vaug[:, kb, hh, :]
                        rkk, rvv = rk[kb // 4], rv[kb // 4]
                    st = {"first": idx == 0, "last": idx == nk - 1, "diag": (kind != "cross") and kb >= 4 * t,
                          "o": kb * 128 - t * TT, "vl": vl, "rvv": rvv, "zb": zb, "kb": kb}
                    mm(pb[zb][:, :], kT[base:base + 64, kcol], qT[base:base + 64, tl(t)], True, kind != "fox",
                       [rkk, rq[t]], [rpb[zb]])
                    return st

                def stZ2(H, st):
                    if kind == "fox":
                        zb, head = st["zb"], H["head"]
                        mm(pb[zb][:, :], selb[:, head * 128:(head + 1) * 128], estrip[:, tl(t)], False, True,
                           [rconst, res_], [rpb[zb]])

                def stA(H, st):
                    diag, o, zb, kb, head = st["diag"], st["o"], st["zb"], st["kb"], H["head"]
                    H["pi"] = (H["pi"] + 1) % 3
                    ps_ = H["pp"][H["pi"]]
                    P = pbf[:, ps_, :]
                    st["ps"] = ps_
                    if kind == "fox":
                        if diag:
                            wa = wslot()
                            S.op("dve", lambda e: e.tensor_scalar(
                                out=work[:, wa, :], in0=pb[zb][:, :], scalar1=ckneg[:, kb * 4 + head:kb * 4 + head + 1], scalar2=80.0,
                                op0=ALU.add, op1=ALU.min), reads=[rpb[zb], rck], writes=[rwk[wa]])
                            S.op("act", lambda e: e.activation(out=P, in_=work[:, wa, :], func=AF.Exp),
                                 reads=[rwk[wa]], writes=[rpbf[ps_]])
                            st["mask"] = lambda: S.op("dve", lambda e: e.tensor_tensor(out=P, in0=P, in1=maskb[:, 385 - o:385 - o + TT], op=ALU.mult),
                                                      reads=[rpbf[ps_], rconst], writes=[rpbf[ps_]])
                        else:
                            S.op("act", lambda e: e.activation(out=P, in_=pb[zb][:, :], func=AF.Exp,
                                                               bias=ckneg[:, kb * 4 + head:kb * 4 + head + 1], scale=1.0),
                                 reads=[rpb[zb], rck], writes=[rpbf[ps_]])
                    else:
                        S.op("act", lambda e: e.activation(out=P, in_=pb[zb][:, :], func=AF.Exp),
                             reads=[rpb[zb]], writes=[rpbf[ps_]])
                        if kind == "dil":
                            d = -o
                            strip, rstrip = H["strip"]
                            S.op("dve", lambda e: e.tensor_tensor(out=P, in0=P, in1=strip[:, d + 384:d + 384 + TT], op=ALU.mult),
                                 reads=[rpbf[ps_]] + rstrip, writes=[rpbf[ps_]])

                def stC(H, st):
                    ps_ = st["ps"]
                    ob = H["ob"]
                    mm(pb[ob][:, :], st["vl"], pbf[:, ps_, :], st["first"], st["last"], [st["rvv"], rpbf[ps_]], [rpb[ob]])

                sts = [[], []]
                for n in range(-1, nk + 2):
                    if 0 <= n + 1 < nk:
                        for H in HD:
                            sts[H["hh"]].append(stZ1(H, n + 1, kbs[n + 1]))
                        for H in HD:
                            stZ2(H, sts[H["hh"]][n + 1])
                    if n == 0 and pend["f"]:
                        for (H, tp, w1) in pend["f"]:
                            fin2(H["hh"], ci, tp, H["ob"], w1, H["zp"][(H["zi"] + 1) % 3])
                        pend["f"] = []
                    if 0 <= n - 1 < nk:
                        for H in HD:
                            mk_ = sts[H["hh"]][n - 1].get("mask")
                            if mk_ is not None:
                                mk_()
                    if 0 <= n < nk:
                        for H in HD:
                            stA(H, sts[H["hh"]][n])
                    if 0 <= n - 2 < nk:
                        for H in HD:
                            stC(H, sts[H["hh"]][n - 2])
                pend["f"] = [(H, t, fin1(H["hh"], H["ob"])) for H in HD]
            for (H, tp, w1) in pend["f"]:
                fin2(H["hh"], ci, tp, H["ob"], w1, H["zp"][(H["zi"] + 1) % 3])
            pend["f"] = []

        cT = mixedT[0:4, 6:8, :].rearrange("p a s -> p (a s)").bitcast(F32)
        lsf = work[0:4, 0:4, :].rearrange("p a s -> p (a s)")

        def w_in_piece(l, col0, n):
            return ("w_in", l, col0, col0 + n)

        def attn_mixer(l, kind, off, mi):
            for hp in range(2):
                wbi = nextw()
                pieces = [(w_in_piece(l, off + hp * 128, 128), 0, 8, 128, 512),
                          (w_in_piece(l, off + 256 + hp * 128, 128), 128, 8, 128, 512),
                          (w_in_piece(l, off + 512 + hp * 128, 128), 256, 8, 128, 512)]
                if kind == "fox" and hp == 0:
                    pieces.append((w_in_piece(l, OFF_F, 4), 384, 8, 4, 512))
                load_w(wbi, pieces)
                drip(4)
                proj_fm(wbi, 512, 0, 128, evac_q)
                proj_fm(wbi, 512, 128, 128, evac_k)
                proj_v(wbi, 512, 256)
                ctr_res = None
                if kind == "fox":
                    ctr_res = [[rm[6][t], rm[7][t]] for t in range(NT)]
                if kind == "fox" and hp == 0:
                    def evac_f(t, b):
                        S.op("act", lambda e: e.activation(out=lsf[:, tl(t)], in_=pb[b][0:4, :], func=AF.Exp, bias=fbn[0:4, l:l + 1], scale=-1.0),
                             reads=[rpb[b], rconst], writes=[rwk[t]])
                        S.op("act", lambda e: e.activation(out=lsf[:, tl(t)], in_=lsf[:, tl(t)], func=AF.Ln, bias=cst[0:4, 1:2], scale=1.0),
                             reads=[rwk[t], rconst], writes=[rwk[t]])
                        S.op("dve", lambda e: e.tensor_scalar(out=lsf[:, tl(t)], in0=lsf[:, tl(t)], scalar1=-0.5, scalar2=None, op0=ALU.mult),
                             reads=[rwk[t]], writes=[rwk[t]])
                    proj_fm(wbi, 512, 384, 4, evac_f)
                    allc = [r for t in range(NT) for r in ctr_res[t]]
                    S.op("dve", lambda e: e.tensor_tensor_scan(out=cT, data0=lsf, data1=lsf, initial=0.0, op0=ALU.add, op1=ALU.add),
                         reads=rwk[0:4], writes=allc)
                    S.op("dve", lambda e: e.tensor_copy(out=estrip[0:4, 0:SEQ], in_=cT), reads=allc, writes=[res_])
                    b = bank()
                    for blk in range(16):
                        S.op("pe", lambda e, blk=blk, b=b: e.transpose(out=pb[b][:, blk * 4:(blk + 1) * 4], in_=cT[0:4, blk * 128:(blk + 1) * 128],
                                                                     identity=ident[0:4, 0:4]),
                             reads=[rconst] + ctr_res[blk // 4], writes=[rpb[b]])
                    S.op("dve", lambda e, b=b: e.tensor_scalar(out=ckneg[:, :], in0=pb[b][:, 0:64], scalar1=-1.0, scalar2=None, op0=ALU.mult),
                         reads=[rpb[b]], writes=[rck])
                if kind == "sb":
                    attn_sb(mi * 2 + hp)
                else:
                    if kind == "dil":
                        wmin["v"] = 3
                        S.dma("sp", estrip[:, :], bass.AP(tpad_d.tensor, (hp * 2) * 128 * TP + 127, [[TP - 1, 128], [1, EW]]),
                              reads=[rtpad], writes=[res_])
                        S.dma("sp", estripB, bass.AP(tpad_d.tensor, (hp * 2 + 1) * 128 * TP + 127, [[TP - 1, 128], [1, EW]]),
                              reads=[rtpad], writes=rwk[0:3])
                    attn2(kind, mi * 2 + hp, hp * 2, ctr_res=ctr_res)
                    wmin["v"] = 0

        def lru(l):
            wbi = nextw()
            load_w(wbi, [(w_in_piece(l, OFF_LX, 512), 0, 8, 512, 512)])
            S.dma("sp", wabd[:, :, :], wabd_d[l].rearrange("c p n -> p c n"), writes=[rwab])
            S.dma("sp", wxbd[:, :, :], wxbd_d[l].rearrange("c p n -> p c n"), writes=[rwab])
            lp = lambda c, k: lrup[:, (l * 2 + c) * 8 + k:(l * 2 + c) * 8 + k + 1]
            for c in range(2):
                S.op("act", lambda e, c=c: e.activation(out=coef[:, c:c + 1], in_=lp(c, 7), func=AF.Exp, scale=-1.0), reads=[rconst], writes=[rcoef])
                S.op("act", lambda e, c=c: e.activation(out=coef[:, c:c + 1], in_=coef[:, c:c + 1], func=AF.Ln, bias=cst[:, 1:2], scale=1.0),
                     reads=[rcoef, rconst], writes=[rcoef])
                S.op("dve", lambda e, c=c: e.tensor_scalar(out=coef[:, c:c + 1], in0=coef[:, c:c + 1], scalar1=-8.0, scalar2=None, op0=ALU.mult),
                     reads=[rcoef], writes=[rcoef])
            raws = [(xr, rxr), (gr, rgr)]

            def unit_steps(c, t):
                raw, rraw = raws[c]
                wA, wB, wC = 3 * c, 3 * c + 1, 3 * c + 2
                pR, pI, pG = 3 * c, 3 * c + 1, 3 * c + 2
                stt = {}
                steps = []

                def s_mm():
                    xb_ = bank()
                    for cc in range(8):
                        mm(pb[xb_][:, :], wsl(wbi, cc, 512, c * 128, 128), hT[:, cc, tl(t)], cc == 0, cc == 7, [rw[wbi], rh[cc][t]], [rpb[xb_]])
                    gb_ = bank()
                    for cc in range(8):
                        mm(pb[gb_][:, :], wsl(wbi, cc, 512, 256 + c * 128, 128), hT[:, cc, tl(t)], cc == 0, cc == 7, [rw[wbi], rh[cc][t]], [rpb[gb_]])
                    stt["xb"] = xb_; stt["gb"] = gb_
                steps.append(s_mm)

                def s_evac():
                    if t > 0:
                        S.op("dve", lambda e: e.tensor_copy(out=raw[:, 0:3], in_=raw[:, 512:515]), reads=rraw, writes=rraw)
                    else:
                        S.op("dve", lambda e: e.memset(raw[:, 0:3], 0.0), writes=rraw)
                    xb_, gb_ = stt["xb"], stt["gb"]
                    S.op("act", lambda e: e.copy(out=raw[:, 3:515], in_=pb[xb_][:, :]), reads=[rpb[xb_]], writes=rraw)
                    S.op("act", lambda e: e.activation(out=pbf[:, pG, :], in_=pb[gb_][:, :], func=AF.Gelu), reads=[rpb[gb_]], writes=[rpbf[pG]])
                steps.append(s_evac)

                def s_conv():
                    S.op("dve", lambda e: e.tensor_scalar(out=work[:, wA, :], in0=raw[:, 3:515], scalar1=lp(c, 3), scalar2=lp(c, 4),
                                                          op0=ALU.mult, op1=ALU.add), reads=rraw + [rconst], writes=[rwk[wA]])
                    for j in range(3):
                        S.op("dve", lambda e, j=j: e.scalar_tensor_tensor(out=work[:, wA, :], in0=raw[:, j:j + TT], scalar=lp(c, j),
                                                                         in1=work[:, wA, :], op0=ALU.mult, op1=ALU.add),
                             reads=rraw + [rconst, rwk[wA]], writes=[rwk[wA]])
                steps.append(s_conv)

                def s_gmm():
                    rb_ = bank()
                    mm(pb[rb_][:, :], wabd[:, c, :], work[:, wA, :], True, True, [rwab, rwk[wA]], [rpb[rb_]])
                    ib_ = bank()
                    mm(pb[ib_][:, :], wxbd[:, c, :], work[:, wA, :], True, True, [rwab, rwk[wA]], [rpb[ib_]])
                    stt["rb"] = rb_; stt["ib"] = ib_
                steps.append(s_gmm)

                def s_sig():
                    rb_, ib_ = stt["rb"], stt["ib"]
                    S.op("act", lambda e: e.activation(out=pbf[:, pR, :], in_=pb[rb_][:, :], func=AF.Sigmoid, bias=lp(c, 5), scale=1.0),
                         reads=[rpb[rb_], rconst], writes=[rpbf[pR]])
                    S.op("act", lambda e: e.activation(out=pbf[:, pI, :], in_=pb[ib_][:, :], func=AF.Sigmoid, bias=lp(c, 6), scale=1.0),
                         reads=[rpb[ib_], rconst], writes=[rpbf[pI]])
                steps.append(s_sig)

                def s_a():
                    S.op("act", lambda e: e.activation(out=work[:, wB, :], in_=pbf[:, pR, :], func=AF.Exp, scale=coef[:, c:c + 1]),
                         reads=[rpbf[pR], rcoef], writes=[rwk[wB]])
                    S.op("dve", lambda e: e.tensor_tensor(out=work[:, wC, :], in0=work[:, wB, :], in1=work[:, wB, :], op=ALU.mult),
                         reads=[rwk[wB]], writes=[rwk[wC]])
                    S.op("dve", lambda e: e.tensor_scalar(out=work[:, wC, :], in0=work[:, wC, :], scalar1=1.0, scalar2=0.0,
                                                          op0=ALU.subtract, op1=ALU.min), reads=[rwk[wC]], writes=[rwk[wC]])
                steps.append(s_a)

                def s_sqrt():
                    S.op("act", lambda e: e.activation(out=work[:, wC, :], in_=work[:, wC, :], func=AF.Sqrt, scale=-1.0),
                         reads=[rwk[wC]], writes=[rwk[wC]])
                    S.op("dve", lambda e: e.tensor_tensor(out=work[:, wC, :], in0=work[:, wC, :], in1=pbf[:, pI, :], op=ALU.mult),
                         reads=[rwk[wC], rpbf[pI]], writes=[rwk[wC]])
                    S.op("dve", lambda e: e.tensor_tensor(out=work[:, wC, :], in0=work[:, wC, :], in1=work[:, wA, :], op=ALU.mult),
                         reads=[rwk[wC], rwk[wA]], writes=[rwk[wC]])
                steps.append(s_sqrt)

                def s_scan():
                    S.op("dve", lambda e: e.tensor_tensor_scan(
                        out=work[:, wA, :], data0=work[:, wB, :], data1=work[:, wC, :],
                        initial=(0.0 if t == 0 else hlast[:, c:c + 1]), op0=ALU.mult, op1=ALU.add),
                        reads=[rwk[wB], rwk[wC], rhl[c]], writes=[rwk[wA]])
                    S.op("dve", lambda e: e.tensor_copy(out=hlast[:, c:c + 1], in_=work[:, wA, TT - 1:TT]),
                         reads=[rwk[wA]], writes=[rhl[c]])
                    S.op("dve", lambda e: e.tensor_tensor(out=mixedT[:, 6 + c, tl(t)], in0=work[:, wA, :], in1=pbf[:, pG, :],
                                                          op=ALU.mult), reads=[rwk[wA], rpbf[pG]], writes=[rm[6 + c][t]])
                steps.append(s_scan)
                return steps

            allsteps = [(unit_steps(0, t), unit_steps(1, t)) for t in range(NT)]
            allsteps[0][0][0](); allsteps[0][1][0]()
            for t in range(NT):
                s0, s1 = allsteps[t]
                for i in range(1, len(s0)):
                    if i == 3 and t + 1 < NT:
                        allsteps[t + 1][0][0](); allsteps[t + 1][1][0]()
                    s0[i]()
                    s1[i]()

        def out_proj(src_d, nk, l_):
            for half in range(2):
                wbi = nextw()
                load_w(wbi, [((src_d, l_, half * 512, (half + 1) * 512), 0, nk, 512, 512)])
                for m4 in range(4):
                    m = half * 4 + m4
                    for t in range(NT):
                        b = bank()
                        for c in range(nk):
                            mm(pb[b][:, :], wsl(wbi, c, 512, m4 * 128, 128), mixedT[:, c, tl(t)], c == 0, c == nk - 1,
                               [rw[wbi], rm[c][t]], [rpb[b]])
                        S.op("dve", lambda e, b=b, m=m, t=t: e.tensor_tensor(out=xT[:, m, tl(t)], in0=pb[b][:, :], in1=xT[:, m, tl(t)], op=ALU.add),
                             reads=[rpb[b], rx[m][t]], writes=[rx[m][t]])

        def cross(l, s):
            rstdm = pbf[:, 5, :].bitcast(F32)
            memT = work[:, 0:4, :].rearrange("p a s -> p (a s)").rearrange("p (c n) -> p c n", c=8)
            S.dma("sp", memT, memT_d[s].rearrange("(c p) n -> p c n", p=128), writes=rwk[0:4])
            S.op("act", lambda e: e.activation(out=memn[:, :, :], in_=memT, func=AF.Square), reads=rwk[0:4], writes=[rmemn])
            b = bank()
            for c in range(8):
                mm(pb[b][:, 0:NMEM], ones_bf[:, :], memn[:, c, :], c == 0, c == 7, [rconst, rmemn], [rpb[b]])
            S.op("act", lambda e, b=b: e.activation(out=rstdm, in_=pb[b][:, 0:NMEM], func=AF.Ln, bias=cst[:, 0:1], scale=1.0 / DM),
                 reads=[rpb[b], rconst], writes=[rpbf[5]])
            S.op("act", lambda e: e.activation(out=rstdm, in_=rstdm, func=AF.Exp, scale=-0.5), reads=[rpbf[5]], writes=[rpbf[5]])
            goff = (l * 4 + G_MEM) * 8
            for c in range(8):
                S.op("dve", lambda e, c=c: e.scalar_tensor_tensor(out=memn[:, c, :], in0=memT[:, c, :], scalar=gains[:, goff + c:goff + c + 1],
                                                                  in1=rstdm, op0=ALU.mult, op1=ALU.mult),
                     reads=rwk[0:4] + [rpbf[5], rconst], writes=[rmemn])
            wmin["v"] = 4
            norm((l * 4 + G_CROSS) * 8)
            wmin["v"] = 0
            wbi = nextw()
            load_w(wbi, [(("w_ck", l, 0, 256), 0, 8, 256, 768), (("w_cv", l, 0, 256), 256, 8, 256, 768), (("w_cq", l, 0, 256), 512, 8, 256, 768)])
            for j in range(2):
                b = bank()
                for c in range(8):
                    mm(pb[b][:, 0:NMEM], wsl(wbi, c, 768, j * 128, 128), memn[:, c, :], c == 0, c == 7, [rw[wbi], rmemn], [rpb[b]])
                S.op("dve", lambda e, b=b, j=j: e.tensor_copy(out=kT[:, j * 256:(j + 1) * 256], in_=pb[b][:, 0:NMEM]), reads=[rpb[b]], writes=[rk[0]])
            for mb in range(2):
                b = bank()
                for c in range(8):
                    mm(pb[b][:, 0:256], memn[:, c, mb * 128:(mb + 1) * 128], wsl(wbi, c, 768, 256, 256), c == 0, c == 7, [rw[wbi], rmemn], [rpb[b]])
                pv = pb[b][:, 0:256].rearrange("p (b n) -> p b n", b=2)
                S.op("dve", lambda e, mb=mb, pv=pv: e.tensor_copy(out=vaug[:, mb * 2:mb * 2 + 2, 0, 0:64], in_=pv[:, :, 0:64]),
                     reads=[rpb[b]], writes=[rv[0]])
                S.op("act", lambda e, mb=mb, pv=pv: e.copy(out=vaug[:, mb * 2:mb * 2 + 2, 1, 64:128], in_=pv[:, :, 64:128]),
                     reads=[rpb[b]], writes=[rv[0]])
            for hp in range(2):
                proj_fm(wbi, 768, 512 + hp * 128, 128, evac_q)
                attn2("cross", hp, hp * 2)
            out_proj("w_co", 2, l)

        def ffn(l):
            norm((l * 4 + G_FFN) * 8)

            def actv(j):
                return mixedT[:, j // 4, tl(j % 4)]

            def ract(j):
                return rm[j // 4][j % 4]
            fp = lambda j, k: ffnp[:, (l * 44 + j) * 4 + k:(l * 44 + j) * 4 + k + 1]
            uraw = [(qT[:, 0:1032].bitcast(F32), rq), (memn[:, :, :].rearrange("p c n -> p (c n)")[:, 0:1032].bitcast(F32), [rmemn])]
            graw = [(kT[:, 0:1032].bitcast(F32), rk), (pbf[:, :, :].rearrange("p a s -> p (a s)")[:, 0:1032].bitcast(F32), rpbf)]
            groups = [(g * 3, min(3, 22 - g * 3)) for g in range(8)]
            par = {"i": 0}

            def front(t, wbi, jj, j):
                par["i"] ^= 1
                p = par["i"]
                ub = bank()
                for c in range(8):
                    mm(pb[ub][:, :], wsl(wbi, c, 768, jj * 128, 128), hT[:, c, tl(t)], c == 0, c == 7, [rw[wbi], rh[c][t]], [rpb[ub]])
                gb_ = bank()
                for c in range(8):
                    mm(pb[gb_][:, :], wsl(wbi, c, 768, 384 + jj * 128, 128), hT[:, c, tl(t)], c == 0, c == 7, [rw[wbi], rh[c][t]], [rpb[gb_]])
                wu = wslot(); wg = wslot()
                for ((raw, rraw), bk, jx, wo) in ((uraw[p], ub, j, wu), (graw[p], gb_, 22 + j, wg)):
                    if t > 0:
                        S.op("pool", lambda e, raw=raw, jx=jx: e.tensor_copy(out=raw[:, 0:2], in_=fhalo[:, jx, :]), reads=[rfh], writes=rraw)
                    else:
                        S.op("pool", lambda e, raw=raw: e.memset(raw[:, 0:2], 0.0), writes=rraw)
                    S.op("act", lambda e, raw=raw, bk=bk: e.copy(out=raw[:, 2:514], in_=pb[bk][:, :]), reads=[rpb[bk]], writes=rraw)
                    S.op("act", lambda e, bk=bk, jx=jx, wo=wo: e.activation(out=work[:, wo, :], in_=pb[bk][:, :], func=AF.Identity,
                                                                            bias=fp(jx, 3), scale=fp(jx, 2)),
                         reads=[rpb[bk], rconst], writes=[rwk[wo]])
                    if t < NT - 1:
                        S.op("pool", lambda e, raw=raw, jx=jx: e.tensor_copy(out=fhalo[:, jx, :], in_=raw[:, 512:514]), reads=rraw, writes=[rfh])
                    for k in range(2):
                        S.op("dve", lambda e, raw=raw, jx=jx, wo=wo, k=k: e.scalar_tensor_tensor(
                            out=work[:, wo, :], in0=raw[:, k:k + TT], scalar=fp(jx, k), in1=work[:, wo, :], op0=ALU.mult, op1=ALU.add),
                            reads=rraw + [rconst, rwk[wo]], writes=[rwk[wo]])
                return (wu, wg, j)

            def back(st):
                wu, wg, j = st
                S.op("act", lambda e: e.activation(out=work[:, wg, :], in_=work[:, wg, :], func=AF.Gelu), reads=[rwk[wg]], writes=[rwk[wg]])
                S.op("dve", lambda e: e.tensor_tensor(out=actv(j), in0=work[:, wu, :], in1=work[:, wg, :], op=ALU.mult),
                     reads=[rwk[wu], rwk[wg]], writes=[ract(j)])

            for t in range(NT):
                pend = None
                for (j0, n) in groups:
                    wbi = nextw()
                    load_w(wbi, [("scr_up", l, j0 // 3)])
                    for jj in range(n):
                        st = front(t, wbi, jj, j0 + jj)
                        if pend is not None:
                            back(pend)
                        pend = st
                back(pend)
                for m in range(8):
                    wbi = nextw()
                    load_w(wbi, [("scr_dn", l, m)])
                    b = bank()
                    for j in range(22):
                        mm(pb[b][:, :], wsl(wbi, j, 128, 0, 128), actv(j), j == 0, j == 21, [rw[wbi], ract(j)], [rpb[b]])
                    S.op("dve", lambda e, b=b, m=m, t=t: e.tensor_tensor(out=xT[:, m, tl(t)], in0=pb[b][:, :], in1=xT[:, m, tl(t)], op=ALU.add),
                         reads=[rpb[b], rx[m][t]], writes=[rx[m][t]])

        for s in range(nseq):
            for c in range(8):
                S.dma("sp", xT[:, c, :], xT_d[s, c * 128:(c + 1) * 128, :], writes=rx[c])
            for l in range(nlayers):
                if s == 0:
                    convq.extend(conv_jobs(l))
                norm((l * 4 + G_MIX) * 8)
                if dbg and s == 0 and l == nlayers - 1:
                    dump("d_h", lambda c, t: hT[:, c, tl(t)], lambda c, t: rh[c][t])
                attn_mixer(l, "sb", OFF_SB, 0)
                attn_mixer(l, "fox", OFF_FOX, 1)
                attn_mixer(l, "dil", OFF_DIL, 2)
                lru(l)
                drip(100)
                if dbg and s == 0 and l == nlayers - 1:
                    dump("d_mixed", lambda c, t: mixedT[:, c, tl(t)], lambda c, t: rm[c][t])
                out_proj("w_out", 8, l)
                if dbg and s == 0 and l == nlayers - 1:
                    dump("d_x1", lambda c, t: xT[:, c, tl(t)], lambda c, t: rx[c][t])
                cross(l, s)
                if dbg and s == 0 and l == nlayers - 1:
                    dump("d_x2", lambda c, t: xT[:, c, tl(t)], lambda c, t: rx[c][t])
                ffn(l)
                if dbg and s == 0 and l == nlayers - 1:
                    dump("d_x3", lambda c, t: xT[:, c, tl(t)], lambda c, t: rx[c][t])
            norm(4 * L * 8, final_seq=s)
        S.finish()
        S.emit()
    return nc


def _t5_bucket(n):
    n = np.maximum(n, 0)
    max_exact = 16
    nf = np.maximum(n, 1).astype(np.float32)
    v = np.log(nf / np.float32(max_exact)) / np.float32(math.log(2048 / max_exact)) * np.float32(32 - max_exact)
    large = max_exact + v.astype(np.int32)
    large = np.minimum(large, 31)
    return np.where(n < max_exact, n, large)


def _host_consts():
    c = {}
    c["ident"] = np.eye(128, dtype=np.float32)
    j = np.arange(128)[:, None]; k = np.arange(128)[None, :]
    c["tri"] = (j >= k).astype(np.float32)
    kk = np.arange(128)[:, None]; jp = np.arange(897)[None, :]
    c["mask"] = (kk + 384 < jp).astype(np.float32)
    i = np.arange(TP); delta = i - 511
    valid = (delta >= 0) & (delta <= 2047)
    bk = _t5_bucket(delta)
    oh = np.zeros((32, TP), np.float32)
    oh[bk[valid], i[valid]] = 1.0
    c["onehot"] = oh
    mult = ((delta <= 128).astype(np.float32) + ((delta % 4 == 0) & (delta <= 512)).astype(np.float32)
            + ((delta % 16 == 0) & (delta <= 2048)).astype(np.float32)) * valid.astype(np.float32)
    c["mult"] = np.ascontiguousarray(np.broadcast_to(mult[None, :], (128, TP))).astype(np.float32)
    sel = np.zeros((4, 4 * 128), np.float32)
    for h in range(4):
        sel[h, h * 128:(h + 1) * 128] = 1.0
    c["sel"] = sel
    return c


def _pcol(vec):
    return np.ascontiguousarray(np.asarray(vec, np.float32).reshape(-1, 128).T)


def _host_params(inp):
    f = lambda a: np.ascontiguousarray(np.asarray(a, dtype=np.float32))
    p = {}
    for nm in ("w_in", "w_out", "w_cq", "w_ck", "w_cv", "w_co", "w_up", "w_down"):
        p[nm] = f(inp[nm])
    g = np.zeros((128, (4 * L + 1) * 8), np.float32)
    for l in range(L):
        for gi, nm in enumerate(("norm_mix_g", "norm_cross_g", "norm_mem_g", "norm_ffn_g")):
            g[:, (l * 4 + gi) * 8:(l * 4 + gi + 1) * 8] = _pcol(inp[nm][l])
    g[:, 4 * L * 8:] = _pcol(inp["final_norm_g"])
    p["gains"] = g
    p["fb"] = f(np.asarray(inp["b_forget"]).T)
    lr = np.zeros((128, L * 2 * 8), np.float32)
    for l in range(L):
        for c in range(2):
            sl = slice(c * 128, (c + 1) * 128)
            base = (l * 2 + c) * 8
            for k in range(4):
                lr[:, base + k] = np.asarray(inp["lru_conv_w"])[l, k, sl]
            lr[:, base + 4] = np.asarray(inp["lru_conv_b"])[l, sl]
            lr[:, base + 5] = np.asarray(inp["lru_b_a"])[l, sl]
            lr[:, base + 6] = np.asarray(inp["lru_b_x"])[l, sl]
            lr[:, base + 7] = np.asarray(inp["lru_lambda"])[l, sl]
    p["lrup"] = lr
    for nm, src in (("wabd", "lru_w_a"), ("wxbd", "lru_w_x")):
        w = np.zeros((L, 2, 128, 128), np.float32)
        a = np.asarray(inp[src], np.float32)
        for l in range(L):
            for c in range(2):
                for g2 in range(2):
                    w[l, c, g2 * 64:(g2 + 1) * 64, g2 * 64:(g2 + 1) * 64] = a[l, c * 2 + g2]
        p[nm] = w
    fp = np.zeros((128, L * 44 * 4), np.float32)
    cw = np.asarray(inp["ffn_conv_w"], np.float32); cb = np.asarray(inp["ffn_conv_b"], np.float32)
    for l in range(L):
        for jx in range(44):
            sl = slice(jx * 128, (jx + 1) * 128)
            base = (l * 44 + jx) * 4
            for k in range(3):
                fp[:, base + k] = cw[l, k, sl]
            fp[:, base + 3] = cb[l, sl]
    p["ffnp"] = fp
    p["relb"] = f(inp["rel_bias"])
    p.update(_host_consts())
    return p


_NC_CACHE = {}


def _get_nc(key, **kw):
    if key not in _NC_CACHE:
        rec = []
        build_nc(rec=rec, **kw)
        _NC_CACHE[key] = build_nc(plan=rec, **kw)
    return _NC_CACHE[key]


def run_cores(inp, ncores=8, dbg=False, nseq=2, nlayers=L):
    shared = _host_params(inp)
    x = np.asarray(inp["x"], np.float32); mem = np.asarray(inp["mem"], np.float32)
    in_maps = []
    for i in range(ncores):
        m = dict(shared)
        m["xT"] = np.ascontiguousarray(x[2 * i:2 * i + 2].transpose(0, 2, 1))
        m["memT"] = np.ascontiguousarray(mem[2 * i:2 * i + 2].transpose(0, 2, 1))
        in_maps.append(m)
    nc = _get_nc((dbg, nseq, nlayers), dbg=dbg, nseq=nseq, nlayers=nlayers)
    res = run_bass_kernel_spmd(nc, in_maps, core_ids=list(range(ncores)))
    return res


def kernel(**inputs):
    res = run_cores(inputs)
    out = np.empty((16, SEQ, DM), np.float32)
    for i, r in enumerate(res.results):
        out[2 * i:2 * i + 2] = np.asarray(r["yT"]).transpose(0, 2, 1)
    return out
```
